# Optimizing a Trainium2 kernel written in Bass

```python
import math
import jax, jax.numpy as jnp
from jax import lax
import numpy as np

D_MODEL = 1024
BATCH = 16
SEQ = 4096
DEPTH = 4

CHUNK = 64
D_PLE = 256
W_BRANCH = D_MODEL // 2
N_BRANCH_COLS = 6 * W_BRANCH
CONV_A_WIDTH = 31
GMLP_BLOCK = 2 * CHUNK
N_HEADS_B = 8
HEAD_DIM_B = W_BRANCH // N_HEADS_B
POOL_WINDOWS = (2, 4, 8, 16)
N_GROUPS_C = len(POOL_WINDOWS)
GROUP_DIM_C = W_BRANCH // N_GROUPS_C
CONV_D_WIDTH = 3
DEEPNORM_ALPHA = (2.0 * DEPTH) ** 0.25
DEEPNORM_BETA = (8.0 * DEPTH) ** -0.25
LN_EPS = 1e-5

kernel_name = 'hybrid_conv_gmlp_pool_shortconv_deepnorm'


def layer_norm(x, g, b):
    xf = x.astype(jnp.float32)
    mu = jnp.mean(xf, axis=-1, keepdims=True)
    var = jnp.mean(jnp.square(xf - mu), axis=-1, keepdims=True)
    y = (xf - mu) * lax.rsqrt(var + LN_EPS)
    return (y * g.astype(jnp.float32) + b.astype(jnp.float32)).astype(x.dtype)


def causal_depthwise_conv(x, w):
    k, c = w.shape
    return lax.conv_general_dilated(
        x, w[:, None, :], window_strides=(1,), padding=[(k - 1, 0)],
        dimension_numbers=('NWC', 'WIO', 'NWC'), feature_group_count=c)


def even_mixer(x, w_in, b_in, conv_w, conv_b, ln_a_g, ln_a_b, ln_v_g, ln_v_b, w_s, b_s, w_out, b_out):
    bt, s, _ = x.shape
    z = jnp.einsum('bsd,de->bse', x, w_in) + b_in
    a_val, a_glu, a_gate, u, v, g_gate = jnp.split(z, 6, axis=-1)
    a = a_val * jax.nn.sigmoid(a_glu)
    a = causal_depthwise_conv(a, conv_w) + conv_b
    a = jax.nn.silu(layer_norm(a, ln_a_g, ln_a_b)) * jax.nn.silu(a_gate)
    u = jax.nn.gelu(u)
    v = layer_norm(jax.nn.gelu(v), ln_v_g, ln_v_b)
    v = v.reshape(bt, s // GMLP_BLOCK, GMLP_BLOCK, N_HEADS_B, HEAD_DIM_B)
    mask = jnp.tril(jnp.ones((GMLP_BLOCK, GMLP_BLOCK), dtype=bool))
    w_s = jnp.where(mask[None], w_s, jnp.zeros_like(w_s))
    sg = jnp.einsum('hts,bnshd->bnthd', w_s, v) + b_s.T[:, :, None]
    g = u * sg.reshape(bt, s, W_BRANCH) * jax.nn.silu(g_gate)
    y = jnp.concatenate([a, g], axis=-1)
    return jnp.einsum('bse,ed->bsd', y, w_out) + b_out


def odd_mixer(x, w_in, b_in, w_pool, pool_scale, conv_w, w_out, b_out):
    bt, s, _ = x.shape
    z = jnp.einsum('bsd,de->bse', x, w_in) + b_in
    c_val, c_gate, d_h, d_b, d_c, d_gate = jnp.split(z, 6, axis=-1)
    vg = c_val.reshape(bt, s, N_GROUPS_C, GROUP_DIM_C)
    cs = jnp.cumsum(vg.astype(jnp.float32), axis=1)
    pos = jnp.arange(1, s + 1, dtype=jnp.float32)
    means = []
    for gi, win in enumerate(POOL_WINDOWS):
        c_g = cs[:, :, gi]
        lag = jnp.pad(c_g, ((0, 0), (win, 0), (0, 0)))[:, :s]
        means.append((c_g - lag) / jnp.minimum(pos, float(win))[:, None])
    pooled = jnp.stack(means, axis=2).astype(vg.dtype) - vg
    c = jnp.einsum('bsgc,gce->bsge', pooled, w_pool).reshape(bt, s, W_BRANCH) * pool_scale
    c = c * jax.nn.silu(c_gate)
    d = d_b * causal_depthwise_conv(d_c * d_h, conv_w)
    d = d * jax.nn.silu(d_gate)
    y = jnp.concatenate([c, d], axis=-1)
    return jnp.einsum('bse,ed->bsd', y, w_out) + b_out


def setup_inputs(seed: int = 0) -> dict:
    key = jax.random.key(seed)
    ks = jax.random.split(key, 26)
    ne = (DEPTH + 1) // 2
    no = DEPTH // 2
    f32 = jnp.float32
    nrm = lambda k, shape, scale: jax.random.normal(k, shape, f32) * scale
    return {
        'x': nrm(ks[0], (BATCH, SEQ, D_MODEL), 1.0),
        'p': nrm(ks[1], (DEPTH, BATCH, SEQ, D_PLE), 1.0),
        'w_in_e': nrm(ks[2], (ne, D_MODEL, N_BRANCH_COLS), D_MODEL ** -0.5),
        'b_in_e': nrm(ks[3], (ne, N_BRANCH_COLS), 0.02),
        'conv_a_w': nrm(ks[4], (ne, CONV_A_WIDTH, W_BRANCH), CONV_A_WIDTH ** -0.5),
        'conv_a_b': nrm(ks[5], (ne, W_BRANCH), 0.02),
        'ln_a_g': 1.0 + nrm(ks[6], (ne, W_BRANCH), 0.05),
        'ln_a_b': nrm(ks[7], (ne, W_BRANCH), 0.02),
        'ln_v_g': 1.0 + nrm(ks[8], (ne, W_BRANCH), 0.05),
        'ln_v_b': nrm(ks[9], (ne, W_BRANCH), 0.02),
        'w_s': nrm(ks[10], (ne, N_HEADS_B, GMLP_BLOCK, GMLP_BLOCK), 0.5 * GMLP_BLOCK ** -0.5),
        'b_s': 1.0 + nrm(ks[11], (ne, N_HEADS_B, GMLP_BLOCK), 0.1),
        'w_out_e': nrm(ks[12], (ne, 2 * W_BRANCH, D_MODEL), DEEPNORM_BETA * (2 * W_BRANCH) ** -0.5),
        'b_out_e': nrm(ks[13], (ne, D_MODEL), 0.02),
        'w_in_o': nrm(ks[14], (no, D_MODEL, N_BRANCH_COLS), D_MODEL ** -0.5),
        'b_in_o': nrm(ks[15], (no, N_BRANCH_COLS), 0.02),
        'w_pool': nrm(ks[16], (no, N_GROUPS_C, GROUP_DIM_C, GROUP_DIM_C), GROUP_DIM_C ** -0.5),
        'pool_scale': 1.0 + nrm(ks[17], (no, W_BRANCH), 0.1),
        'conv_d_w': nrm(ks[18], (no, CONV_D_WIDTH, W_BRANCH), CONV_D_WIDTH ** -0.5),
        'w_out_o': nrm(ks[19], (no, 2 * W_BRANCH, D_MODEL), DEEPNORM_BETA * (2 * W_BRANCH) ** -0.5),
        'b_out_o': nrm(ks[20], (no, D_MODEL), 0.02),
        'ln_g': 1.0 + nrm(ks[21], (DEPTH, D_MODEL), 0.05),
        'ln_b': nrm(ks[22], (DEPTH, D_MODEL), 0.02),
        'w_ple': nrm(ks[23], (DEPTH, D_PLE, D_MODEL), D_PLE ** -0.5),
        'w_ple_gate': nrm(ks[24], (DEPTH, D_MODEL, D_MODEL), D_MODEL ** -0.5),
        'b_ple_gate': nrm(ks[25], (DEPTH, D_MODEL), 0.02),
    }


def reference(x, p, w_in_e, b_in_e, conv_a_w, conv_a_b, ln_a_g, ln_a_b, ln_v_g, ln_v_b, w_s, b_s,
              w_out_e, b_out_e, w_in_o, b_in_o, w_pool, pool_scale, conv_d_w, w_out_o, b_out_o,
              ln_g, ln_b, w_ple, w_ple_gate, b_ple_gate):
    for i in range(DEPTH):
        j = i // 2
        if i % 2 == 0:
            out = even_mixer(x, w_in_e[j], b_in_e[j], conv_a_w[j], conv_a_b[j], ln_a_g[j], ln_a_b[j],
                             ln_v_g[j], ln_v_b[j], w_s[j], b_s[j], w_out_e[j], b_out_e[j])
        else:
            out = odd_mixer(x, w_in_o[j], b_in_o[j], w_pool[j], pool_scale[j], conv_d_w[j],
                            w_out_o[j], b_out_o[j])
        h = layer_norm(DEEPNORM_ALPHA * x + out, ln_g[i], ln_b[i])
        gate = jax.nn.sigmoid(jnp.einsum('bsd,de->bse', h, w_ple_gate[i]) + b_ple_gate[i])
        x = h + gate * jnp.einsum('bsk,kd->bsd', p[i], w_ple[i])
    return x
```

```python
import contextlib
import os
import numpy as np
import concourse.bass as bass
import concourse.mybir as mybir
from concourse.bass_utils import run_bass_kernel_spmd

F32 = mybir.dt.float32
F32R = mybir.dt.float32r
BF16 = mybir.dt.bfloat16
AF = mybir.ActivationFunctionType
ALU = mybir.AluOpType

NCORES = 8
D = 1024
NCH = 8
TT = 1024
ST = 512
NSUB = TT // ST
DEPTH = 4
ALPHA = (2.0 * DEPTH) ** 0.25
EPS = 1e-5
NSLOT = 4
KPRE = NSLOT - 1
NTR = 7
NTRR = 3
POOL_WINDOWS = (2, 4, 8, 16)


SYNC_SAME = os.environ.get('KSYNC_SAME', '1') == '1'
SYNC_NOSNAP = os.environ.get('KSYNC_NOSNAP', '0') == '1'


class Sched:
    ENGS = ('pe', 'act', 'dve', 'pool', 'sp')

    def __init__(self, nc):
        self.nc = nc
        self.prog = {e: [] for e in self.ENGS}
        self.cnt = {}
        self.known = {e: {} for e in self.ENGS}
        self.snap = {}
        self.last_write = {}
        self.readers = {}
        self.semkeys = list(self.ENGS)

    def new_sem(self, name):
        self.semkeys.append(name)
        return name

    def _deps(self, reads, writes):
        deps = []
        for b in reads:
            t = self.last_write.get(b)
            if t is not None:
                deps.append(t)
        for b in writes:
            t = self.last_write.get(b)
            if t is not None:
                deps.append(t)
            deps.extend(self.readers.get(b, ()))
        return deps

    def _emit_waits(self, e, deps):
        kn = self.known[e]
        need = {}
        for (k, v) in deps:
            if k == e and not SYNC_SAME:
                continue
            if kn.get(k, 0) >= v:
                continue
            if need.get(k, 0) < v:
                need[k] = v
        for k, v in need.items():
            if kn.get(k, 0) >= v:
                continue
            self.prog[e].append(('wait', k, v))
            kn[k] = v
            sn = self.snap.get((k, v))
            if sn and not SYNC_NOSNAP:
                for kk, vv in sn.items():
                    if kn.get(kk, 0) < vv:
                        kn[kk] = vv

    def _record(self, tok, reads, writes):
        for b in reads:
            self.readers.setdefault(b, []).append(tok)
        for b in writes:
            self.last_write[b] = tok
            self.readers[b] = []

    def op(self, e, fn, reads=(), writes=(), extra=()):
        deps = self._deps(reads, writes) + list(extra)
        self._emit_waits(e, deps)
        n = self.cnt.get(e, 0) + 1
        self.cnt[e] = n
        self.prog[e].append(('op', fn, e, 1))
        tok = (e, n)
        self.snap[tok] = dict(self.known[e])
        self._record(tok, reads, writes)
        return tok

    def dma(self, q, semkey, fn, reads=(), writes=(), extra=()):
        deps = self._deps(reads, writes) + list(extra)
        self._emit_waits(q, deps)
        n = self.cnt.get(semkey, 0) + 16
        self.cnt[semkey] = n
        self.prog[q].append(('op', fn, semkey, 16))
        tok = (semkey, n)
        self._record(tok, reads, writes)
        return tok

    def wait_tokens(self, e, toks):
        self._emit_waits(e, list(toks))

    def emit(self, st):
        nc = self.nc
        semh = {}
        for k in self.semkeys:
            semh[k] = st.enter_context(nc.semaphore("s_" + str(k)))
        block = st.enter_context(nc.Block())
        engmap = {'pe': 'tensor', 'act': 'scalar', 'dve': 'vector', 'pool': 'gpsimd', 'sp': 'sync'}

        def make(e):
            def body(eng):
                for it in self.prog[e]:
                    if it[0] == 'wait':
                        eng.wait_ge(semh[it[1]], it[2])
                    else:
                        ins = it[1](eng)
                        ins.then_inc(semh[it[2]], it[3])
            return body
        for e in self.ENGS:
            getattr(block, engmap[e])(make(e))


def cols_layout():
    off = {}
    n = 0

    def add(name, w):
        nonlocal n
        off[name] = n
        n += w
    for j in range(2):
        add('bin_e%d' % j, 24)
        add('cab%d' % j, 4)
        add('lag%d' % j, 4)
        add('lab%d' % j, 4)
        add('bout_e%d' % j, 8)
        add('caw%d' % j, 124)
        add('lvg%d' % j, 4)
        add('lvb%d' % j, 4)
        add('bin_o%d' % j, 24)
        add('psc%d' % j, 4)
        add('cdw%d' % j, 12)
        add('bout_o%d' % j, 8)
    for i in range(DEPTH):
        add('lng%d' % i, 8)
        add('lnb%d' % i, 8)
        add('bpg%d' % i, 8)
    return off, n


COLS, NCOL = cols_layout()
CST_IDENT = 0
CST_MASK = 128
CST_INV = 256
CST_EPS = 320
NCST = 321


def _colpack(v):
    v = np.asarray(v, dtype=np.float32)
    return np.ascontiguousarray(v.reshape(-1, 128).T)


def host_prep(inp):
    cols = np.zeros((128, NCOL), np.float32)

    def put(name, arr):
        cols[:, COLS[name]:COLS[name] + arr.shape[1]] = arr
    for j in range(2):
        put('bin_e%d' % j, _colpack(inp['b_in_e'][j]))
        put('cab%d' % j, _colpack(inp['conv_a_b'][j]))
        put('lag%d' % j, _colpack(inp['ln_a_g'][j]))
        put('lab%d' % j, _colpack(inp['ln_a_b'][j]))
        put('bout_e%d' % j, _colpack(inp['b_out_e'][j]))
        caw = np.asarray(inp['conv_a_w'][j], np.float32).reshape(31, 4, 128)
        put('caw%d' % j, np.ascontiguousarray(caw.transpose(2, 0, 1)).reshape(128, 124))
        put('lvg%d' % j, _colpack(inp['ln_v_g'][j]))
        put('lvb%d' % j, _colpack(inp['ln_v_b'][j]))
        put('bin_o%d' % j, _colpack(inp['b_in_o'][j]))
        put('psc%d' % j, _colpack(inp['pool_scale'][j]))
        cdw = np.asarray(inp['conv_d_w'][j], np.float32).reshape(3, 4, 128)
        put('cdw%d' % j, np.ascontiguousarray(cdw.transpose(2, 0, 1)).reshape(128, 12))
        put('bout_o%d' % j, _colpack(inp['b_out_o'][j]))
    for i in range(DEPTH):
        put('lng%d' % i, _colpack(inp['ln_g'][i]))
        put('lnb%d' % i, _colpack(inp['ln_b'][i]))
        put('bpg%d' % i, _colpack(inp['b_ple_gate'][i]))
    rows = np.zeros((128, 2, 512), np.float32)
    for j in range(2):
        rows[:, j, :] = np.asarray(inp['b_in_e'][j], np.float32)[4 * 512:5 * 512][None, :]
    wsT = np.ascontiguousarray(np.asarray(inp['w_s'], np.float32).transpose(3, 0, 1, 2))
    b_s = np.asarray(inp['b_s'], np.float32)
    bs = np.ascontiguousarray(np.repeat(b_s.reshape(2, 4, 2, 1, 128), 64, axis=3).reshape(2, 4, 128, 128).transpose(2, 0, 1, 3))
    cst = np.zeros((128, NCST), np.float32)
    cst[:, CST_IDENT:CST_IDENT + 128] = np.eye(128, dtype=np.float32)
    cst[:, CST_MASK:CST_MASK + 128] = np.triu(np.ones((128, 128), np.float32))
    for g, w in enumerate(POOL_WINDOWS):
        for t in range(16):
            cst[:, CST_INV + g * 16 + t] = np.float32(1.0) / np.float32(min(t + 1, w))
    cst[:, CST_EPS] = EPS
    onesr = np.zeros((128, 256), np.float32)
    onesr[:, 0:128] = 1.0 / 1024.0
    onesr[:, 128:256] = 1.0 / 512.0
    shared = {
        'cols': cols, 'rows': rows, 'wsT': wsT, 'bs': bs, 'cst': cst, 'onesr': onesr,
        'w_in_e': np.ascontiguousarray(inp['w_in_e'], np.float32),
        'w_in_o': np.ascontiguousarray(inp['w_in_o'], np.float32),
        'w_out_e': np.ascontiguousarray(inp['w_out_e'], np.float32),
        'w_out_o': np.ascontiguousarray(inp['w_out_o'], np.float32),
        'w_ple_gate': np.ascontiguousarray(inp['w_ple_gate'], np.float32),
        'w_ple': np.ascontiguousarray(inp['w_ple'], np.float32),
        'w_pool': np.ascontiguousarray(inp['w_pool'], np.float32),
    }
    return shared


class Builder:
    def __init__(self, ntiles, tiles_per_seq, layers):
        self.ntiles = ntiles
        self.tps = tiles_per_seq
        self.layers = list(layers)
        self.ntok = ntiles * TT

    def layer_units(self, i):
        j = i // 2
        u = []
        if i % 2 == 0:
            w = ('w_in_e', j)
            for c in range(4):
                u.append((('A', c), [(w, 0 * 512 + c * 128, 128), (w, 1 * 512 + c * 128, 128)]))
            u.append((('V', 0), [(w, 4 * 512, 256)]))
            u.append((('V', 1), [(w, 4 * 512 + 256, 256)]))
            for c in range(4):
                u.append((('UG', c), [(w, 3 * 512 + c * 128, 128), (w, 5 * 512 + c * 128, 128)]))
            for h in range(2):
                u.append((('AG', h), [(w, 2 * 512 + h * 256, 256)]))
            wo = ('w_out_e', j)
        else:
            w = ('w_in_o', j)
            for h in range(2):
                u.append((('CV', h), [(w, 0 * 512 + h * 256, 256)]))
            for c in range(4):
                u.append((('HC', c), [(w, 2 * 512 + c * 128, 128), (w, 4 * 512 + c * 128, 128)]))
            for h in range(2):
                u.append((('CG', h), [(w, 1 * 512 + h * 256, 256)]))
            for c in range(4):
                u.append((('BG', c), [(w, 3 * 512 + c * 128, 128), (w, 5 * 512 + c * 128, 128)]))
            wo = ('w_out_o', j)
        for h in range(4):
            u.append((('WO', h), [(wo, h * 256, 256)]))
        for h in range(4):
            u.append((('WG', h), [(('w_ple_gate', i), h * 256, 256)]))
        return u

    def build(self):
        nc = bass.Bass("TRN2", target_bir_lowering=False)
        nc.dge_precook = False
        self.nc = nc
        ntok = self.ntok
        dr = {}
        dr['xT'] = nc.dram_tensor("xT", [D, ntok], F32R, kind="ExternalInput").ap()
        dr['pT'] = nc.dram_tensor("pT", [DEPTH, 256, ntok], F32R, kind="ExternalInput").ap()
        for name, shp in (('w_in_e', [2, 1024, 3072]), ('w_in_o', [2, 1024, 3072]), ('w_out_e', [2, 1024, 1024]),
                          ('w_out_o', [2, 1024, 1024]), ('w_ple_gate', [4, 1024, 1024]), ('w_ple', [4, 256, 1024]),
                          ('w_pool', [2, 4, 128, 128]), ('onesr', [128, 256])):
            dr[name] = nc.dram_tensor(name, shp, F32R, kind="ExternalInput").ap()
        for name, shp in (('cols', [128, NCOL]), ('rows', [128, 2, 512]), ('wsT', [128, 2, 8, 128]),
                          ('bs', [128, 2, 4, 128]), ('cst', [128, NCST])):
            dr[name] = nc.dram_tensor(name, shp, F32, kind="ExternalInput").ap()
        dr['outT'] = nc.dram_tensor("outT", [D, ntok], F32, kind="ExternalOutput").ap()
        self.dr = dr

        with contextlib.ExitStack() as st:
            def sb(name, shape, dt):
                return st.enter_context(nc.sbuf_tensor("sb_" + name, shape, dt))
            self.x32 = sb("x32", [128, NCH, TT], F32)
            self.y32 = sb("y32", [128, NCH, TT], F32)
            self.pbuf = sb("pbuf", [128, 2, TT], F32)
            self.wple = sb("wple", [128, 2, 1024], F32)
            self.wring = sb("wring", [128, NSLOT, 8, 256], F32)
            ARENA = 24192
            self.arena = sb("arena", [128, ARENA], BF16)
            self.rows = sb("rows", [128, 2, 512], F32)
            self.onesb = sb("onesb", [128, 64], BF16)
            self.sm2 = sb("sm2", [128, 2, 8], F32)
            self.cols = sb("cols", [128, NCOL], F32)
            self.wsTb = sb("wsTb", [128, 2, 8, 128], BF16)
            self.bs = sb("bs", [128, 2, 4, 128], F32)
            self.wpool = sb("wpool", [128, 2, 4, 128], F32)
            self.cst = sb("cst", [128, NCST], F32)
            self.onesr = sb("onesr", [128, 256], F32)
            self.stA = sb("stA", [128, 2, 4, 30], BF16)
            self.stV = sb("stV", [128, 2, 4, 16], F32)
            self.stD = sb("stD", [128, 2, 4, 2], BF16)
            self.tr = [sb("tr%d" % k, [128, ST], F32) for k in range(NTR)]
            self.trr = [sb("trr%d" % k, [128, ST], F32) for k in range(NTRR)]
            self.pt = [sb("pt%d" % k, [128, 16 + ST], F32) for k in range(2)]
            self.sm = sb("sm", [128, 8, 8], F32)
            self.ps = [st.enter_context(nc.psum_tensor("ps%d" % k, [128, ST], F32)) for k in range(8)]
            ar = self.arena
            self.a_bf = ar[:, 0:4 * 1054].rearrange("p (c t) -> p c t", c=4)
            self.vln = ar[:, 4216:4216 + 4096].rearrange("p (b f) -> p b f", b=8)
            self.diag = ar[:, 8312:8312 + 124 * 128].rearrange("p (k n) -> p k n", k=124)
            self.vbuf = ar[:, 0:8320].bitcast(F32).rearrange("p (c t) -> p c t", c=4)
            self.dch = ar[:, 8320:8320 + 4 * 1026].rearrange("p (c t) -> p c t", c=4)
            self.diag3 = ar[:, 12424:12424 + 12 * 128].rearrange("p (k n) -> p k n", k=12)

            self.bufs = {'A': self.x32, 'B': self.y32}
            self.xn = 'A'
            self.yn = 'B'
            self.S = Sched(nc)
            self.gen()
            self.S.emit(st)
        return nc

    def T(self):
        k = self.tr_i % NTR
        self.tr_i += 1
        return self.tr[k], ('tr', k)

    def TR(self):
        k = self.trr_i % NTRR
        self.trr_i += 1
        return self.trr[k], ('trr', k)

    def bank(self, ring):
        lst = {'z': (0, 1, 2, 3), 'aux': (4, 5), 'st': (6, 7)}[ring]
        k = self.bank_i[ring] % len(lst)
        self.bank_i[ring] += 1
        b = lst[k]
        return self.ps[b], ('ps', b)

    def col(self, name, k=0):
        o = COLS[name] + k
        return self.cols[:, o:o + 1]

    def arena_extra(self):
        return list(self.prev_arena.items())

    def aop(self, e, fn, reads=(), writes=()):
        tok = self.S.op(e, fn, reads=reads, writes=writes, extra=self.arena_extra())
        if self.cur_arena.get(tok[0], 0) < tok[1]:
            self.cur_arena[tok[0]] = tok[1]
        return tok

    def arena_next_layer(self):
        for k, v in self.cur_arena.items():
            if self.prev_arena.get(k, 0) < v:
                self.prev_arena[k] = v
        self.cur_arena = {}

    def issue_upto(self, n):
        S = self.S
        n = min(n, len(self.units) - 1)
        while self.uissued <= n:
            m = self.uissued
            tag, segs = self.units[m]
            slot = m % NSLOT
            off = 0
            for (wname, widx), c0, nc_ in segs:
                src = self.dr[wname][widx].rearrange("(kc p) n -> p kc n", p=128)[:, :, c0:c0 + nc_]
                dst = self.wring[:, slot, :, off:off + nc_].bitcast(F32R)
                S.dma('sp', 'w%d' % slot, lambda e, dst=dst, src=src: e.dma_start(out=dst, in_=src),
                      writes=[('wslot', slot)])
                off += nc_
            self.uissued += 1

    def take_unit(self, tag, keep=False):
        if not keep:
            self.close_all()
        n = self.ucur
        assert self.units[n][0] == tag, (self.units[n][0], tag)
        assert self.uissued > n, "unit not issued yet"
        self.ucur += 1
        self.open.append(n)
        return n % NSLOT

    def close_unit(self, n):
        assert self.open[0] == n
        self.open.pop(0)
        self.issue_upto(n + NSLOT)

    def close_all(self):
        while self.open:
            self.close_unit(self.open[0])

    def xk(self, c, s):
        return (self.xn, c, s)

    def yk(self, c, s):
        return (self.yn, c, s)

    def mkview(self, name):
        buf = self.bufs[name]
        return lambda c, s: buf[:, c, s * ST:(s + 1) * ST]

    def zmm(self, slot, coff, s):
        bk, bkey = self.bank('z')
        X = self.mkview(self.xn)

        def fn(e, slot=slot, coff=coff, s=s, bk=bk):
            ins = None
            for k in range(NCH):
                ins = e.matmul(bk[:], self.wring[:, slot, k, coff:coff + 128].bitcast(F32R),
                               X(k, s).bitcast(F32R), start=(k == 0), stop=(k == NCH - 1))
            return ins
        self.S.op('pe', fn, reads=[('wslot', slot)] + [self.xk(k, s) for k in range(NCH)], writes=[bkey])
        return bk, bkey

    def gen(self):
        S = self.S
        self.tr_i = 0
        self.trr_i = 0
        self.bank_i = {'z': 0, 'aux': 0, 'st': 0}
        self.prev_arena = {}
        self.cur_arena = {}
        for k in range(NSLOT):
            S.new_sem('w%d' % k)
        for nm in ('ld_p', 'ld_wple'):
            S.new_sem(nm)
        for k in range(NCH * NSUB):
            S.new_sem('ld_x%d' % k)
        for k in range(NCH * NSUB):
            S.new_sem('st_out%d' % k)
        self.units = []
        for ti in range(self.ntiles):
            for i in self.layers:
                self.units.extend(self.layer_units(i))
        self.ucur = 0
        self.uissued = 0
        self.open = []
        self.issue_upto(NSLOT - 1)

        dr = self.dr

        def once(name, fn, writes):
            S.new_sem('ld_' + name)
            S.dma('pool', 'ld_' + name, fn, writes=writes)
        once('cols', lambda e: e.dma_start(out=self.cols[:], in_=dr['cols']), ['cols'])
        once('cst', lambda e: e.dma_start(out=self.cst[:], in_=dr['cst']), ['cst'])
        once('onesr', lambda e: e.dma_start(out=self.onesr[:].bitcast(F32R), in_=dr['onesr']), ['onesr'])
        once('bs', lambda e: e.dma_start(out=self.bs[:], in_=dr['bs']), ['bs'])
        once('rows', lambda e: e.dma_start(out=self.rows[:], in_=dr['rows']), ['rows'])
        once('wpool', lambda e: e.dma_start(out=self.wpool[:].bitcast(F32R),
                                            in_=dr['w_pool'].rearrange("j g c e -> c j g e")), ['wpool'])
        mask = self.cst[:, CST_MASK:CST_MASK + 128]
        for j in range(2):
            for q in range(2):
                t, tk = self.T()
                tv = t[:].rearrange("p (h t) -> p h t", h=4)
                once('wsT%d%d' % (j, q), lambda e, tv=tv, j=j, q=q: e.dma_start(out=tv, in_=dr['wsT'][:, j, q * 4:(q + 1) * 4, :]), [tk])
                S.op('dve', lambda e, tv=tv, j=j, q=q: e.tensor_tensor(
                    out=self.wsTb[:, j, q * 4:(q + 1) * 4, :], in0=tv,
                    in1=mask.unsqueeze(1).to_broadcast([128, 4, 128]), op=ALU.mult),
                    reads=[tk, 'cst'], writes=['wsTb'])

        S.op('dve', lambda e: e.memset(self.onesb[:], 1.0), writes=['onesb'])
        for j in range(2):
            bk, bkey = self.bank('z')

            def fn(e, j=j, bk=bk):
                ins = None
                for c in range(4):
                    for hh in range(2):
                        ins = e.matmul(bk[hh * 64:(hh + 1) * 64, c * 128:(c + 1) * 128], self.onesb[:, :],
                                       self.wsTb[:, j, 2 * c + hh, :], start=True, stop=True)
                return ins
            S.op('pe', fn, reads=['onesb', 'wsTb'], writes=[bkey])
            for c in range(4):
                S.op('dve', lambda e, j=j, c=c, bk=bk: e.scalar_tensor_tensor(
                    out=self.bs[:, j, c, :], in0=bk[:, c * 128:(c + 1) * 128], scalar=self.col('lvb%d' % j, c),
                    in1=self.bs[:, j, c, :], op0=ALU.mult, op1=ALU.add), reads=[bkey, 'cols', 'bs'], writes=['bs'])

        for ti in range(self.ntiles):
            self.ti = ti
            self.t0 = ti * TT
            self.seq_start = (ti % self.tps == 0)
            if ti == 0:
                for sx in range(NSUB):
                    self.gen_xload(0, sx)
            for li, i in enumerate(self.layers):
                self.last_layer = (li == len(self.layers) - 1)
                self.arena_next_layer()
                self.gen_layer(i)
        S.wait_tokens('pool', [('st_out%d' % k, S.cnt.get('st_out%d' % k, 0)) for k in range(NCH * NSUB)])

    def gen_xload(self, ti, sx):
        S = self.S
        dr = self.dr
        t0 = ti * TT
        xb = self.bufs[self.xn]
        for c in range(NCH):
            S.dma('pool', 'ld_x%d' % (c * NSUB + sx), lambda e, c=c, sx=sx, t0=t0, xb=xb: e.dma_start(
                out=xb[:, c, sx * ST:(sx + 1) * ST].bitcast(F32R),
                in_=dr['xT'][c * 128:(c + 1) * 128, t0 + sx * ST:t0 + (sx + 1) * ST]),
                writes=[self.xk(c, sx)])

    def gen_layer(self, i):
        S = self.S
        dr = self.dr
        j = i // 2
        t0 = self.t0
        S.dma('pool', 'ld_p', lambda e, i=i, t0=t0: e.dma_start(
            out=self.pbuf[:].bitcast(F32R), in_=dr['pT'][i].rearrange("(k p) t -> p k t", p=128)[:, :, t0:t0 + TT]),
            writes=['pbuf'])
        S.dma('pool', 'ld_wple', lambda e, i=i: e.dma_start(
            out=self.wple[:].bitcast(F32R), in_=dr['w_ple'][i].rearrange("(k p) n -> p k n", p=128)),
            writes=['wple'])
        if i % 2 == 0:
            self.gen_even(i, j)
        else:
            self.gen_odd(i, j)
        self.gen_tail(i, j)
        if not self.last_layer:
            self.xn, self.yn = self.yn, self.xn

    def gen_even(self, i, j):
        S = self.S
        X = self.mkview(self.xn)
        Y = self.mkview(self.yn)
        xbuf = self.bufs[self.xn]
        dr = self.dr
        bin_ = 'bin_e%d' % j
        ident = self.cst[:, CST_IDENT:CST_IDENT + 128]
        if self.seq_start:
            self.aop('pool', lambda e: e.memset(self.a_bf[:, :, 0:30], 0.0), writes=['abf_head'])
        else:
            self.aop('pool', lambda e, j=j: e.tensor_copy(out=self.a_bf[:, :, 0:30], in_=self.stA[:, j]),
                     reads=[('stA', j)], writes=['abf_head'])
        for idx in range(124):
            self.aop('pool', lambda e, idx=idx, j=j: e.tensor_scalar(
                out=self.diag[:, idx, :], in0=ident, scalar1=self.col('caw%d' % j, idx), scalar2=0.0,
                op0=ALU.mult, op1=ALU.add), reads=['cols', 'cst'], writes=[('diag', idx % 4)])

        for c in range(4):
            slot = self.take_unit(('A', c))
            for s in range(NSUB):
                bg, bgk = self.zmm(slot, 128, s)
                sg, sgk = self.T()
                S.op('act', lambda e, bg=bg, sg=sg, c=c: e.activation(
                    out=sg[:], in_=bg[:], func=AF.Sigmoid, bias=self.col(bin_, 4 + c), scale=1.0),
                    reads=[bgk, 'cols'], writes=[sgk])
                bv, bvk = self.zmm(slot, 0, s)
                self.aop('dve', lambda e, bv=bv, sg=sg, c=c, s=s: e.scalar_tensor_tensor(
                    out=self.a_bf[:, c, 30 + s * ST:30 + (s + 1) * ST], in0=bv[:], scalar=self.col(bin_, c),
                    in1=sg[:], op0=ALU.add, op1=ALU.mult),
                    reads=[bvk, sgk, 'cols'], writes=[('abf', c, s)])
        self.aop('pool', lambda e, j=j: e.tensor_copy(out=self.stA[:, j], in_=self.a_bf[:, :, TT:TT + 30]),
                 reads=[('abf', c, 1) for c in range(4)], writes=[('stA', j)])

        slot0 = self.take_unit(('V', 0))
        slot1 = self.take_unit(('V', 1), keep=True)
        BV = self.rows[:, j, :]
        for b in range(8):
            s = b // 4
            g = b // 4
            bk, bkey = self.bank('z')

            def fn(e, b=b, bk=bk):
                ins = None
                for h, sl in ((0, slot0), (1, slot1)):
                    for k in range(NCH):
                        ins = e.matmul(bk[:, h * 256:(h + 1) * 256],
                                       xbuf[:, k, b * 128:(b + 1) * 128].bitcast(F32R),
                                       self.wring[:, sl, k, :].bitcast(F32R), start=(k == 0), stop=(k == NCH - 1))
                return ins
            S.op('pe', fn, reads=[('wslot', slot0), ('wslot', slot1)] + [self.xk(k, s) for k in range(NCH)], writes=[bkey])
            t1, t1k = self.T()
            S.op('dve', lambda e, bk=bk, t1=t1: e.tensor_tensor(out=t1[:], in0=bk[:], in1=BV, op=ALU.add),
                 reads=[bkey, 'rows'], writes=[t1k])
            self.aop('act', lambda e, t1=t1, b=b: e.activation(out=self.vln[:, b, :], in_=t1[:], func=AF.Gelu_apprx_tanh),
                     reads=[t1k], writes=[('vln', b)])
            smb = self.sm[:, b, :]
            smk = ('sm', b)
            self.aop('dve', lambda e, b=b, smb=smb: e.bn_stats(out=smb[:, 0:6], in_=self.vln[:, b, :]),
                     reads=[('vln', b)], writes=[smk])
            S.op('dve', lambda e, smb=smb: e.bn_aggr(out=smb[:, 6:8], in_=smb[:, 0:6]), reads=[smk], writes=[smk])
            if b % 4 == 3:
                gk = ('sm2', g)
                blks = range(4 * g, 4 * g + 4)
                S.op('act', lambda e, g=g: e.activation(
                    out=self.sm2[:, g, 0:4], in_=self.sm[:, 4 * g:4 * g + 4, 7], func=AF.Sqrt,
                    bias=self.cst[:, CST_EPS:CST_EPS + 1], scale=1.0),
                    reads=[('sm', bb) for bb in blks] + ['cst'], writes=[gk])
                S.op('dve', lambda e, g=g: e.reciprocal(out=self.sm2[:, g, 4:8], in_=self.sm2[:, g, 0:4]),
                     reads=[gk], writes=[gk])
                for bi, bb in enumerate(blks):
                    self.aop('dve', lambda e, g=g, bi=bi, bb=bb: e.tensor_scalar(
                        out=self.vln[:, bb, :], in0=self.vln[:, bb, :], scalar1=self.sm[:, bb, 6:7],
                        scalar2=self.sm2[:, g, 4 + bi:5 + bi], op0=ALU.subtract, op1=ALU.mult),
                        reads=[gk, ('sm', bb), ('vln', bb)], writes=[('vln', bb)])

        ones512 = self.onesr[:, 128:256].bitcast(F32R)
        for s in range(NSUB):
            s1b, s1k = self.bank('st')
            s2b, s2k = self.bank('st')
            pend = []

            def stat_mm(c, sqt, sqk, s=s, s1b=s1b, s2b=s2b, s1k=s1k, s2k=s2k):
                def fn(e):
                    e.matmul(s1b[:], ones512, Y(c, s).bitcast(F32R), start=(c == 0), stop=(c == 3))
                    return e.matmul(s2b[:], ones512, sqt[:].bitcast(F32R), start=(c == 0), stop=(c == 3))
                S.op('pe', fn, reads=[self.yk(c, s), sqk, 'onesr'], writes=[s1k, s2k])
            for c in range(4):
                ab, abk = self.bank('aux')

                def fn(e, c=c, s=s, ab=ab):
                    ins = None
                    for k in range(31):
                        ins = e.matmul(ab[:], self.diag[:, k * 4 + c, :], self.a_bf[:, c, s * ST + k:s * ST + k + ST],
                                       start=(k == 0), stop=(k == 30))
                    return ins
                rd = [('diag', c), ('abf', c, s), ('abf', c, s - 1) if s > 0 else 'abf_head']
                self.aop('pe', fn, reads=rd, writes=[abk])
                S.op('act', lambda e, c=c, s=s, ab=ab: e.activation(
                    out=Y(c, s).bitcast(F32R), in_=ab[:], func=AF.Identity, bias=self.col('cab%d' % j, c), scale=1.0),
                    reads=[abk, 'cols'], writes=[self.yk(c, s)])
                sq, sqk = self.TR()
                S.op('act', lambda e, c=c, ab=ab, sq=sq: e.activation(
                    out=sq[:].bitcast(F32R), in_=ab[:], func=AF.Square, bias=self.col('cab%d' % j, c), scale=1.0),
                    reads=[abk, 'cols'], writes=[sqk])
                pend.append((c, sq, sqk))
                if len(pend) > 1:
                    stat_mm(*pend.pop(0))
            while pend:
                stat_mm(*pend.pop(0))
            self.ln_small(s1b, s1k, s2b, s2k)
            for c in range(4):
                t, tk = self.T()
                S.op('dve', lambda e, c=c, s=s, t=t, s2b=s2b: e.tensor_tensor(
                    out=t[:], in0=Y(c, s).bitcast(F32), in1=s2b[:], op=ALU.mult), reads=[self.yk(c, s), s2k], writes=[tk])
                S.op('dve', lambda e, t=t, s1b=s1b: e.tensor_tensor(out=t[:], in0=t[:], in1=s1b[:], op=ALU.add),
                     reads=[tk, s1k], writes=[tk])
                S.op('act', lambda e, c=c, s=s, t=t: e.activation(
                    out=Y(c, s).bitcast(F32R), in_=t[:], func=AF.Silu, bias=self.col('lab%d' % j, c),
                    scale=self.col('lag%d' % j, c)), reads=[tk, 'cols'], writes=[self.yk(c, s)])

        for c in range(4):
            slot = self.take_unit(('UG', c))
            bus = []
            abs_ = []
            bgs = []
            for s in range(NSUB):
                bus.append(self.zmm(slot, 0, s))
            for s in range(NSUB):
                ab, abk = self.bank('aux')

                def fn(e, c=c, s=s, ab=ab):
                    ins = None
                    for hh in range(2):
                        h = 2 * c + hh
                        for bb in range(4):
                            b = s * 4 + bb
                            ins = e.matmul(ab[hh * 64:(hh + 1) * 64, bb * 128:(bb + 1) * 128],
                                           self.vln[:, b, h * 64:(h + 1) * 64], self.wsTb[:, j, h, :],
                                           start=True, stop=True)
                    return ins
                self.aop('pe', fn, reads=[('vln', s * 4 + bb) for bb in range(4)] + ['wsTb'], writes=[abk])
                abs_.append((ab, abk))
            for s in range(NSUB):
                bgs.append(self.zmm(slot, 128, s))
            t1s = []
            t2s = []
            for s in range(NSUB):
                bu, buk = bus[s]
                t1, t1k = self.T()
                S.op('act', lambda e, bu=bu, t1=t1, c=c: e.activation(
                    out=t1[:], in_=bu[:], func=AF.Gelu_apprx_tanh, bias=self.col(bin_, 12 + c), scale=1.0),
                    reads=[buk, 'cols'], writes=[t1k])
                t1s.append((t1, t1k))
            for s in range(NSUB):
                bgt, bgk = bgs[s]
                t2, t2k = self.T()
                S.op('act', lambda e, bgt=bgt, t2=t2, c=c: e.activation(
                    out=t2[:], in_=bgt[:], func=AF.Silu, bias=self.col(bin_, 20 + c), scale=1.0),
                    reads=[bgk, 'cols'], writes=[t2k])
                t2s.append((t2, t2k))
            t3s = []
            for s in range(NSUB):
                ab, abk = abs_[s]
                t3, t3k = self.T()
                S.op('dve', lambda e, ab=ab, t3=t3, c=c: e.scalar_tensor_tensor(
                    out=t3[:].rearrange("p (b t) -> p b t", b=4), in0=ab[:].rearrange("p (b t) -> p b t", b=4),
                    scalar=self.col('lvg%d' % j, c),
                    in1=self.bs[:, j, c, :].unsqueeze(1).to_broadcast([128, 4, 128]), op0=ALU.mult, op1=ALU.add),
                    reads=[abk, 'bs', 'cols'], writes=[t3k])
                t3s.append((t3, t3k))
            for s in range(NSUB):
                t1, t1k = t1s[s]
                t3, t3k = t3s[s]
                S.op('dve', lambda e, t1=t1, t3=t3: e.tensor_tensor(out=t3[:], in0=t3[:], in1=t1[:], op=ALU.mult),
                     reads=[t1k, t3k], writes=[t3k])
            for s in range(NSUB):
                t2, t2k = t2s[s]
                t3, t3k = t3s[s]
                S.op('dve', lambda e, t2=t2, t3=t3, c=c, s=s: e.tensor_tensor(
                    out=Y(4 + c, s).bitcast(F32R), in0=t3[:], in1=t2[:], op=ALU.mult),
                    reads=[t2k, t3k], writes=[self.yk(4 + c, s)])

        for h in range(2):
            slot = self.take_unit(('AG', h))
            for s in range(NSUB):
                for cc in range(2):
                    c = 2 * h + cc
                    bk, bkey = self.zmm(slot, cc * 128, s)
                    t, tk = self.T()
                    S.op('act', lambda e, bk=bk, t=t, c=c: e.activation(
                        out=t[:], in_=bk[:], func=AF.Silu, bias=self.col(bin_, 8 + c), scale=1.0),
                        reads=[bkey, 'cols'], writes=[tk])
                    S.op('dve', lambda e, t=t, c=c, s=s: e.tensor_tensor(
                        out=Y(c, s).bitcast(F32R), in0=Y(c, s).bitcast(F32), in1=t[:], op=ALU.mult),
                        reads=[tk, self.yk(c, s)], writes=[self.yk(c, s)])

    def ln_small(self, s1b, s1k, s2b, s2k):
        S = self.S
        t, tk = self.T()
        S.op('act', lambda e, t=t, s1b=s1b: e.activation(out=t[:], in_=s1b[:], func=AF.Square), reads=[s1k], writes=[tk])
        S.op('dve', lambda e, t=t, s2b=s2b: e.tensor_tensor(out=t[:], in0=s2b[:], in1=t[:], op=ALU.subtract),
             reads=[s2k, tk], writes=[tk])
        S.op('act', lambda e, t=t: e.activation(out=t[:], in_=t[:], func=AF.Sqrt, bias=self.cst[:, CST_EPS:CST_EPS + 1], scale=1.0),
             reads=[tk, 'cst'], writes=[tk])
        S.op('dve', lambda e, t=t, s2b=s2b: e.reciprocal(out=s2b[:], in_=t[:]), reads=[tk], writes=[s2k])
        S.op('dve', lambda e, t=t, s1b=s1b: e.tensor_scalar(out=t[:], in0=s1b[:], scalar1=-1.0, scalar2=None, op0=ALU.mult),
             reads=[s1k], writes=[tk])
        S.op('dve', lambda e, t=t, s1b=s1b, s2b=s2b: e.tensor_tensor(out=s1b[:], in0=s2b[:], in1=t[:], op=ALU.mult),
             reads=[tk, s2k], writes=[s1k])

    def gen_odd(self, i, j):
        S = self.S
        X = self.mkview(self.xn)
        Y = self.mkview(self.yn)
        xbuf = self.bufs[self.xn]
        bin_ = 'bin_o%d' % j
        ident = self.cst[:, CST_IDENT:CST_IDENT + 128]
        if self.seq_start:
            self.aop('pool', lambda e: e.memset(self.vbuf[:, :, 0:16], 0.0), writes=['vb_head'])
            self.aop('pool', lambda e: e.memset(self.dch[:, :, 0:2], 0.0), writes=['dch_head'])
        else:
            self.aop('pool', lambda e, j=j: e.tensor_copy(out=self.vbuf[:, :, 0:16], in_=self.stV[:, j]),
                     reads=[('stV', j)], writes=['vb_head'])
            self.aop('pool', lambda e, j=j: e.tensor_copy(out=self.dch[:, :, 0:2], in_=self.stD[:, j]),
                     reads=[('stD', j)], writes=['dch_head'])
        for idx in range(12):
            self.aop('pool', lambda e, idx=idx, j=j: e.tensor_scalar(
                out=self.diag3[:, idx, :], in0=ident, scalar1=self.col('cdw%d' % j, idx), scalar2=0.0,
                op0=ALU.mult, op1=ALU.add), reads=['cols', 'cst'], writes=[('diag3', idx % 4)])

        for h in range(2):
            slot = self.take_unit(('CV', h))
            for s in range(NSUB):
                for cc in range(2):
                    c = 2 * h + cc
                    bk, bkey = self.zmm(slot, cc * 128, s)
                    self.aop('act', lambda e, bk=bk, c=c, s=s: e.activation(
                        out=self.vbuf[:, c, 16 + s * ST:16 + (s + 1) * ST], in_=bk[:], func=AF.Identity,
                        bias=self.col(bin_, c), scale=1.0), reads=[bkey, 'cols'], writes=[('vb', c, s)])
        self.aop('pool', lambda e, j=j: e.tensor_copy(out=self.stV[:, j], in_=self.vbuf[:, :, TT:TT + 16]),
                 reads=[('vb', c, 1) for c in range(4)], writes=[('stV', j)])

        for c in range(4):
            slot = self.take_unit(('HC', c))
            for s in range(NSUB):
                bh, bhk = self.zmm(slot, 0, s)
                t, tk = self.T()
                S.op('act', lambda e, bh=bh, t=t, c=c: e.activation(
                    out=t[:], in_=bh[:], func=AF.Identity, bias=self.col(bin_, 8 + c), scale=1.0),
                    reads=[bhk, 'cols'], writes=[tk])
                bc, bck = self.zmm(slot, 128, s)
                self.aop('dve', lambda e, bc=bc, t=t, c=c, s=s: e.scalar_tensor_tensor(
                    out=self.dch[:, c, 2 + s * ST:2 + (s + 1) * ST], in0=bc[:], scalar=self.col(bin_, 16 + c),
                    in1=t[:], op0=ALU.add, op1=ALU.mult), reads=[bck, tk, 'cols'], writes=[('dch', c, s)])
        self.aop('pool', lambda e, j=j: e.tensor_copy(out=self.stD[:, j], in_=self.dch[:, :, TT:TT + 2]),
                 reads=[('dch', c, 1) for c in range(4)], writes=[('stD', j)])

        for s in range(NSUB):
            for c in range(4):
                W = POOL_WINDOWS[c]
                lo = s * ST
                src = self.vbuf[:, c, lo:lo + 16 + ST]
                rdk = [('vb', c, s), ('vb', c, s - 1) if s > 0 else 'vb_head']
                cur = src
                curk = None
                sh = 1
                nstep = c + 1
                for stp in range(nstep):
                    dst = self.pt[stp % 2]
                    dk = ('pt', stp % 2)
                    a0 = 16 - (W - 2 * sh) if stp < nstep - 1 else 16
                    a0 = max(a0, sh)
                    n = 16 + ST - a0
                    rds = list(rdk) if curk is None else [curk]
                    self.aop('dve', lambda e, dst=dst, cur=cur, a0=a0, n=n, sh=sh: e.tensor_tensor(
                        out=dst[:, a0:a0 + n], in0=cur[:, a0:a0 + n], in1=cur[:, a0 - sh:a0 - sh + n], op=ALU.add),
                        reads=rds, writes=[dk])
                    cur = dst
                    curk = dk
                    sh *= 2
                pl, plk = self.TR()
                self.aop('dve', lambda e, cur=cur, pl=pl, src=src, W=W: e.scalar_tensor_tensor(
                    out=pl[:].bitcast(F32R), in0=cur[:, 16:16 + ST], scalar=1.0 / W, in1=src[:, 16:16 + ST],
                    op0=ALU.mult, op1=ALU.subtract), reads=[curk] + rdk, writes=[plk])
                if self.seq_start and s == 0:
                    inv = self.cst[:, CST_INV + c * 16:CST_INV + (c + 1) * 16]
                    self.aop('dve', lambda e, cur=cur, inv=inv: e.tensor_tensor(
                        out=cur[:, 16:32], in0=cur[:, 16:32], in1=inv, op=ALU.mult), reads=[curk, 'cst'], writes=[curk])
                    self.aop('dve', lambda e, cur=cur, pl=pl, src=src: e.tensor_tensor(
                        out=pl[:, 0:16].bitcast(F32R), in0=cur[:, 16:32], in1=src[:, 16:32], op=ALU.subtract),
                        reads=[curk] + rdk, writes=[plk])
                ab, abk = self.bank('aux')
                S.op('pe', lambda e, ab=ab, pl=pl, c=c: e.matmul(
                    ab[:], self.wpool[:, j, c, :].bitcast(F32R), pl[:].bitcast(F32R), start=True, stop=True),
                    reads=[plk, 'wpool'], writes=[abk])
                S.op('act', lambda e, ab=ab, c=c, s=s: e.activation(
                    out=Y(c, s).bitcast(F32R), in_=ab[:], func=AF.Identity, scale=self.col('psc%d' % j, c)),
                    reads=[abk, 'cols'], writes=[self.yk(c, s)])

        for s in range(NSUB):
            for c in range(4):
                ab, abk = self.bank('aux')

                def fn(e, c=c, s=s, ab=ab):
                    ins = None
                    for k in range(3):
                        ins = e.matmul(ab[:], self.diag3[:, k * 4 + c, :], self.dch[:, c, s * ST + k:s * ST + k + ST],
                                       start=(k == 0), stop=(k == 2))
                    return ins
                rd = [('diag3', c), ('dch', c, s), ('dch', c, s - 1) if s > 0 else 'dch_head']
                self.aop('pe', fn, reads=rd, writes=[abk])
                S.op('act', lambda e, ab=ab, c=c, s=s: e.activation(
                    out=Y(4 + c, s).bitcast(F32R), in_=ab[:], func=AF.Identity),
                    reads=[abk], writes=[self.yk(4 + c, s)])

        for h in range(2):
            slot = self.take_unit(('CG', h))
            for s in range(NSUB):
                for cc in range(2):
                    c = 2 * h + cc
                    bk, bkey = self.zmm(slot, cc * 128, s)
                    t, tk = self.T()
                    S.op('act', lambda e, bk=bk, t=t, c=c: e.activation(
                        out=t[:], in_=bk[:], func=AF.Silu, bias=self.col(bin_, 4 + c), scale=1.0),
                        reads=[bkey, 'cols'], writes=[tk])
                    S.op('dve', lambda e, t=t, c=c, s=s: e.tensor_tensor(
                        out=Y(c, s).bitcast(F32R), in0=Y(c, s).bitcast(F32), in1=t[:], op=ALU.mult),
                        reads=[tk, self.yk(c, s)], writes=[self.yk(c, s)])

        for c in range(4):
            slot = self.take_unit(('BG', c))
            for s in range(NSUB):
                bb_, bbk = self.zmm(slot, 0, s)
                bg, bgk = self.zmm(slot, 128, s)
                t1, t1k = self.T()
                S.op('act', lambda e, bg=bg, t1=t1, c=c: e.activation(
                    out=t1[:], in_=bg[:], func=AF.Silu, bias=self.col(bin_, 20 + c), scale=1.0),
                    reads=[bgk, 'cols'], writes=[t1k])
                S.op('dve', lambda e, bb_=bb_, t1=t1, c=c: e.scalar_tensor_tensor(
                    out=t1[:], in0=bb_[:], scalar=self.col(bin_, 12 + c), in1=t1[:], op0=ALU.add, op1=ALU.mult),
                    reads=[bbk, t1k, 'cols'], writes=[t1k])
                S.op('dve', lambda e, t1=t1, c=c, s=s: e.tensor_tensor(
                    out=Y(4 + c, s).bitcast(F32R), in0=Y(4 + c, s).bitcast(F32), in1=t1[:], op=ALU.mult),
                    reads=[t1k, self.yk(4 + c, s)], writes=[self.yk(4 + c, s)])

    def gen_tail(self, i, j):
        S = self.S
        X = self.mkview(self.xn)
        Y = self.mkview(self.yn)
        xbuf = self.bufs[self.xn]
        dr = self.dr
        bout = ('bout_e%d' if i % 2 == 0 else 'bout_o%d') % j
        ones1024 = self.onesr[:, 0:128].bitcast(F32R)
        for s in range(NSUB):
            for c in range(NCH):
                S.op('act', lambda e, c=c, s=s: e.activation(
                    out=X(c, s).bitcast(F32R), in_=X(c, s), func=AF.Identity, bias=self.col(bout, c), scale=ALPHA),
                    reads=[self.xk(c, s), 'cols'], writes=[self.xk(c, s)])
        stb = {}
        for s in range(NSUB):
            ring = 'st' if s == 0 else 'aux'
            b1, k1 = self.bank(ring)
            b2, k2 = self.bank(ring)
            stb[s] = (b1, k1, b2, k2)
        pend = []

        def stat_mm(c, s, sqt, sqk):
            b1, k1, b2, k2 = stb[s]

            def fn(e):
                e.matmul(b1[:], ones1024, X(c, s).bitcast(F32R), start=(c == 0), stop=(c == NCH - 1))
                return e.matmul(b2[:], ones1024, sqt[:].bitcast(F32R), start=(c == 0), stop=(c == NCH - 1))
            S.op('pe', fn, reads=[self.xk(c, s), sqk, 'onesr'], writes=[k1, k2])
        wo_slots = []
        wo_n = []
        for h in range(4):
            wo_n.append(self.ucur)
            wo_slots.append(self.take_unit(('WO', h), keep=(h > 0)))

        def flush():
            while pend:
                stat_mm(*pend.pop(0))

        def wo_pass(s):
            for h in range(4):
                slot = wo_slots[h]
                for cc in range(2):
                    c = 2 * h + cc
                    bk, bkey = self.bank('z')

                    def fn(e, slot=slot, cc=cc, s=s, bk=bk):
                        ins = None
                        for k in range(NCH):
                            ins = e.matmul(bk[:], self.wring[:, slot, k, cc * 128:(cc + 1) * 128].bitcast(F32R),
                                           Y(k, s).bitcast(F32R), start=(k == 0), stop=(k == NCH - 1))
                        return ins
                    S.op('pe', fn, reads=[('wslot', slot)] + [self.yk(k, s) for k in range(NCH)], writes=[bkey])
                    S.op('dve', lambda e, bk=bk, c=c, s=s: e.tensor_tensor(
                        out=X(c, s).bitcast(F32R), in0=bk[:], in1=X(c, s), op=ALU.add),
                        reads=[bkey, self.xk(c, s)], writes=[self.xk(c, s)])
                    sq, sqk = self.TR()
                    S.op('act', lambda e, sq=sq, c=c, s=s: e.activation(
                        out=sq[:].bitcast(F32R), in_=X(c, s), func=AF.Square), reads=[self.xk(c, s)], writes=[sqk])
                    pend.append((c, s, sq, sqk))
                    if len(pend) > 2:
                        stat_mm(*pend.pop(0))
                    yield
                if s == NSUB - 1:
                    self.close_unit(wo_n[h])

        def run(g):
            for _ in g:
                pass

        def interleave(ga, gb):
            da = db = False
            while not (da and db):
                if not da:
                    try:
                        next(ga)
                    except StopIteration:
                        da = True
                if not db:
                    try:
                        next(gb)
                    except StopIteration:
                        db = True

        run(wo_pass(0))
        flush()
        interleave(wo_pass(1), self.gen_ln_apply(i, 0, stb, X))
        flush()

        wg_slots = []
        wg_n = []
        for h in range(4):
            wg_n.append(self.ucur)
            wg_slots.append(self.take_unit(('WG', h), keep=(h > 0)))

        def gate_pass(s):
            for h in range(4):
                slot = wg_slots[h]
                for cc in range(2):
                    c = 2 * h + cc
                    bk, bkey = self.zmm(slot, cc * 128, s)
                    gt, gtk = self.T()
                    S.op('act', lambda e, bk=bk, gt=gt, c=c: e.activation(
                        out=gt[:], in_=bk[:], func=AF.Sigmoid, bias=self.col('bpg%d' % i, c), scale=1.0),
                        reads=[bkey, 'cols'], writes=[gtk])
                    bp, bpk = self.bank('z')

                    def fn(e, c=c, s=s, bp=bp):
                        ins = None
                        for k in range(2):
                            ins = e.matmul(bp[:], self.wple[:, k, c * 128:(c + 1) * 128].bitcast(F32R),
                                           self.pbuf[:, k, s * ST:(s + 1) * ST].bitcast(F32R), start=(k == 0), stop=(k == 1))
                        return ins
                    S.op('pe', fn, reads=['wple', 'pbuf'], writes=[bpk])
                    S.op('dve', lambda e, bp=bp, gt=gt: e.tensor_tensor(out=gt[:], in0=bp[:], in1=gt[:], op=ALU.mult),
                         reads=[bpk, gtk], writes=[gtk])
                    S.op('dve', lambda e, gt=gt, c=c, s=s: e.tensor_tensor(
                        out=Y(c, s).bitcast(F32R), in0=X(c, s), in1=gt[:], op=ALU.add),
                        reads=[gtk, self.xk(c, s)], writes=[self.yk(c, s)])
                    if self.last_layer:
                        t0 = self.t0
                        S.dma('pool', 'st_out%d' % (c * NSUB + s), lambda e, c=c, s=s, t0=t0: e.dma_start(
                            out=dr['outT'][c * 128:(c + 1) * 128, t0 + s * ST:t0 + (s + 1) * ST], in_=Y(c, s)),
                            reads=[self.yk(c, s)])
                    yield
                if s == NSUB - 1:
                    self.close_unit(wg_n[h])

        interleave(gate_pass(0), self.gen_ln_apply(i, 1, stb, X))
        if self.last_layer and self.ti + 1 < self.ntiles:
            self.gen_xload(self.ti + 1, 0)
        run(gate_pass(1))
        if self.last_layer and self.ti + 1 < self.ntiles:
            self.gen_xload(self.ti + 1, 1)

    def gen_ln_apply(self, i, s, stb, X):
        S = self.S
        b1, k1, b2, k2 = stb[s]
        self.ln_small(b1, k1, b2, k2)
        yield
        for c in range(NCH):
            t, tk = self.T()
            S.op('dve', lambda e, c=c, s=s, t=t, b2=b2: e.tensor_tensor(
                out=t[:], in0=X(c, s), in1=b2[:], op=ALU.mult), reads=[self.xk(c, s), k2], writes=[tk])
            S.op('dve', lambda e, t=t, b1=b1: e.tensor_tensor(out=t[:], in0=t[:], in1=b1[:], op=ALU.add),
                 reads=[tk, k1], writes=[tk])
            S.op('act', lambda e, c=c, s=s, t=t: e.activation(
                out=X(c, s).bitcast(F32R), in_=t[:], func=AF.Identity, bias=self.col('lnb%d' % i, c),
                scale=self.col('lng%d' % i, c)), reads=[tk, 'cols'], writes=[self.xk(c, s)])
            yield


_NC_CACHE = {}


def get_nc(ntiles, tps, layers):
    key = (ntiles, tps, tuple(layers))
    if key not in _NC_CACHE:
        _NC_CACHE[key] = Builder(ntiles, tps, layers).build()
    return _NC_CACHE[key]


def run_cores(inp, x_cores, p_cores, ntiles, tps, layers):
    shared = host_prep(inp)
    nc = get_nc(ntiles, tps, layers)
    in_maps = []
    for xc, pc in zip(x_cores, p_cores):
        m = dict(shared)
        m['xT'] = np.ascontiguousarray(np.asarray(xc, np.float32).T)
        m['pT'] = np.ascontiguousarray(np.asarray(pc, np.float32).transpose(0, 2, 1))
        in_maps.append(m)
    res = run_bass_kernel_spmd(nc, in_maps, core_ids=list(range(len(in_maps))))
    return [np.ascontiguousarray(r['outT'].T) for r in res.results]


def kernel(**inputs):
    x = np.asarray(inputs['x'], np.float32)
    p = np.asarray(inputs['p'], np.float32)
    B, SEQ, _ = x.shape
    bpc = B // NCORES
    x_cores = [x[k * bpc:(k + 1) * bpc].reshape(bpc * SEQ, D) for k in range(NCORES)]
    p_cores = [p[:, k * bpc:(k + 1) * bpc].reshape(DEPTH, bpc * SEQ, 256) for k in range(NCORES)]
    outs = run_cores(inputs, x_cores, p_cores, ntiles=bpc * SEQ // TT, tps=SEQ // TT, layers=range(DEPTH))
    out = np.stack([o.reshape(bpc, SEQ, D) for o in outs], axis=0).reshape(B, SEQ, D)
    return out.astype(np.float32)
```

```python
import contextlib
import os
import numpy as np
import concourse.bass as bass
import concourse.mybir as mybir
from concourse.bass_utils import run_bass_kernel_spmd

F32 = mybir.dt.float32
F32R = mybir.dt.float32r
BF16 = mybir.dt.bfloat16
AF = mybir.ActivationFunctionType
ALU = mybir.AluOpType

NCORES = 8
D = 1024
NCH = 8
TT = 1024
ST = 512
NSUB = TT // ST
DEPTH = 4
ALPHA = (2.0 * DEPTH) ** 0.25
EPS = 1e-5
NSLOT = 4
KPRE = NSLOT - 1
NTR = 7
NTRR = 4
POOL_WINDOWS = (2, 4, 8, 16)


SYNC_SAME = os.environ.get('KSYNC_SAME', '1') == '1'
SYNC_NOSNAP = os.environ.get('KSYNC_NOSNAP', '0') == '1'


class Sched:
    ENGS = ('pe', 'act', 'dve', 'pool', 'sp')

    def __init__(self, nc):
        self.nc = nc
        self.prog = {e: [] for e in self.ENGS}
        self.cnt = {}
        self.known = {e: {} for e in self.ENGS}
        self.snap = {}
        self.last_write = {}
        self.readers = {}
        self.semkeys = list(self.ENGS)

    def new_sem(self, name):
        self.semkeys.append(name)
        return name

    def _deps(self, reads, writes):
        deps = []
        for b in reads:
            t = self.last_write.get(b)
            if t is not None:
                deps.append(t)
        for b in writes:
            t = self.last_write.get(b)
            if t is not None:
                deps.append(t)
            deps.extend(self.readers.get(b, ()))
        return deps

    def _emit_waits(self, e, deps):
        kn = self.known[e]
        need = {}
        for (k, v) in deps:
            if k == e and not SYNC_SAME:
                continue
            if kn.get(k, 0) >= v:
                continue
            if need.get(k, 0) < v:
                need[k] = v
        for k, v in need.items():
            if kn.get(k, 0) >= v:
                continue
            self.prog[e].append(('wait', k, v))
            kn[k] = v
            sn = self.snap.get((k, v))
            if sn and not SYNC_NOSNAP:
                for kk, vv in sn.items():
                    if kn.get(kk, 0) < vv:
                        kn[kk] = vv

    def _record(self, tok, reads, writes):
        for b in reads:
            self.readers.setdefault(b, []).append(tok)
        for b in writes:
            self.last_write[b] = tok
            self.readers[b] = []

    def op(self, e, fn, reads=(), writes=(), extra=()):
        deps = self._deps(reads, writes) + list(extra)
        self._emit_waits(e, deps)
        n = self.cnt.get(e, 0) + 1
        self.cnt[e] = n
        self.prog[e].append(('op', fn, e, 1))
        tok = (e, n)
        self.snap[tok] = dict(self.known[e])
        self._record(tok, reads, writes)
        return tok

    def dma(self, q, semkey, fn, reads=(), writes=(), extra=()):
        deps = self._deps(reads, writes) + list(extra)
        self._emit_waits(q, deps)
        n = self.cnt.get(semkey, 0) + 16
        self.cnt[semkey] = n
        self.prog[q].append(('op', fn, semkey, 16))
        tok = (semkey, n)
        self._record(tok, reads, writes)
        return tok

    def wait_tokens(self, e, toks):
        self._emit_waits(e, list(toks))

    def emit(self, st):
        nc = self.nc
        semh = {}
        for k in self.semkeys:
            semh[k] = st.enter_context(nc.semaphore("s_" + str(k)))
        block = st.enter_context(nc.Block())
        engmap = {'pe': 'tensor', 'act': 'scalar', 'dve': 'vector', 'pool': 'gpsimd', 'sp': 'sync'}

        def make(e):
            def body(eng):
                for it in self.prog[e]:
                    if it[0] == 'wait':
                        eng.wait_ge(semh[it[1]], it[2])
                    else:
                        ins = it[1](eng)
                        ins.then_inc(semh[it[2]], it[3])
            return body
        for e in self.ENGS:
            getattr(block, engmap[e])(make(e))


def cols_layout():
    off = {}
    n = 0

    def add(name, w):
        nonlocal n
        off[name] = n
        n += w
    for j in range(2):
        add('bin_e%d' % j, 24)
        add('cab%d' % j, 4)
        add('lag%d' % j, 4)
        add('lab%d' % j, 4)
        add('bout_e%d' % j, 8)
        add('caw%d' % j, 124)
        add('lvg%d' % j, 4)
        add('lvb%d' % j, 4)
        add('bin_o%d' % j, 24)
        add('psc%d' % j, 4)
        add('cdw%d' % j, 12)
        add('bout_o%d' % j, 8)
    for i in range(DEPTH):
        add('lng%d' % i, 8)
        add('lnb%d' % i, 8)
        add('bpg%d' % i, 8)
    return off, n


COLS, NCOL = cols_layout()
CST_IDENT = 0
CST_MASK = 128
CST_INV = 256
CST_EPS = 320
NCST = 321


def _colpack(v):
    v = np.asarray(v, dtype=np.float32)
    return np.ascontiguousarray(v.reshape(-1, 128).T)


def host_prep(inp):
    cols = np.zeros((128, NCOL), np.float32)

    def put(name, arr):
        cols[:, COLS[name]:COLS[name] + arr.shape[1]] = arr
    for j in range(2):
        put('bin_e%d' % j, _colpack(inp['b_in_e'][j]))
        put('cab%d' % j, _colpack(inp['conv_a_b'][j]))
        put('lag%d' % j, _colpack(inp['ln_a_g'][j]))
        put('lab%d' % j, _colpack(inp['ln_a_b'][j]))
        put('bout_e%d' % j, _colpack(inp['b_out_e'][j]))
        caw = np.asarray(inp['conv_a_w'][j], np.float32).reshape(31, 4, 128)
        put('caw%d' % j, np.ascontiguousarray(caw.transpose(2, 0, 1)).reshape(128, 124))
        put('lvg%d' % j, _colpack(inp['ln_v_g'][j]))
        put('lvb%d' % j, _colpack(inp['ln_v_b'][j]))
        put('bin_o%d' % j, _colpack(inp['b_in_o'][j]))
        put('psc%d' % j, _colpack(inp['pool_scale'][j]))
        cdw = np.asarray(inp['conv_d_w'][j], np.float32).reshape(3, 4, 128)
        put('cdw%d' % j, np.ascontiguousarray(cdw.transpose(2, 0, 1)).reshape(128, 12))
        put('bout_o%d' % j, _colpack(inp['b_out_o'][j]))
    for i in range(DEPTH):
        put('lng%d' % i, _colpack(inp['ln_g'][i]))
        put('lnb%d' % i, _colpack(inp['ln_b'][i]))
        put('bpg%d' % i, _colpack(inp['b_ple_gate'][i]))
    rows = np.zeros((128, 2, 512), np.float32)
    for j in range(2):
        rows[:, j, :] = np.asarray(inp['b_in_e'][j], np.float32)[4 * 512:5 * 512][None, :]
    wsT = np.ascontiguousarray(np.asarray(inp['w_s'], np.float32).transpose(3, 0, 1, 2))
    b_s = np.asarray(inp['b_s'], np.float32)
    bs = np.ascontiguousarray(np.repeat(b_s.reshape(2, 4, 2, 1, 128), 64, axis=3).reshape(2, 4, 128, 128).transpose(2, 0, 1, 3))
    cst = np.zeros((128, NCST), np.float32)
    cst[:, CST_IDENT:CST_IDENT + 128] = np.eye(128, dtype=np.float32)
    cst[:, CST_MASK:CST_MASK + 128] = np.triu(np.ones((128, 128), np.float32))
    for g, w in enumerate(POOL_WINDOWS):
        for t in range(16):
            cst[:, CST_INV + g * 16 + t] = np.float32(1.0) / np.float32(min(t + 1, w))
    cst[:, CST_EPS] = EPS
    onesr = np.zeros((128, 256), np.float32)
    onesr[:, 0:128] = 1.0 / 1024.0
    onesr[:, 128:256] = 1.0 / 512.0
    shared = {
        'cols': cols, 'rows': rows, 'wsT': wsT, 'bs': bs, 'cst': cst, 'onesr': onesr,
        'w_in_e': np.ascontiguousarray(inp['w_in_e'], np.float32),
        'w_in_o': np.ascontiguousarray(inp['w_in_o'], np.float32),
        'w_out_e': np.ascontiguousarray(inp['w_out_e'], np.float32),
        'w_out_o': np.ascontiguousarray(inp['w_out_o'], np.float32),
        'w_ple_gate': np.ascontiguousarray(inp['w_ple_gate'], np.float32),
        'w_ple': np.ascontiguousarray(inp['w_ple'], np.float32),
        'w_pool': np.ascontiguousarray(inp['w_pool'], np.float32),
    }
    return shared


class Builder:
    def __init__(self, ntiles, tiles_per_seq, layers):
        self.ntiles = ntiles
        self.tps = tiles_per_seq
        self.layers = list(layers)
        self.ntok = ntiles * TT

    def layer_units(self, i):
        j = i // 2
        u = []
        if i % 2 == 0:
            w = ('w_in_e', j)
            for c in range(4):
                u.append((('A', c), [(w, 0 * 512 + c * 128, 128), (w, 1 * 512 + c * 128, 128)]))
            u.append((('V', 0), [(w, 4 * 512, 256)]))
            u.append((('V', 1), [(w, 4 * 512 + 256, 256)]))
            for c in range(4):
                u.append((('UG', c), [(w, 3 * 512 + c * 128, 128), (w, 5 * 512 + c * 128, 128)]))
            for h in range(2):
                u.append((('AG', h), [(w, 2 * 512 + h * 256, 256)]))
            wo = ('w_out_e', j)
        else:
            w = ('w_in_o', j)
            for h in range(2):
                u.append((('CV', h), [(w, 0 * 512 + h * 256, 256)]))
            for c in range(4):
                u.append((('HC', c), [(w, 2 * 512 + c * 128, 128), (w, 4 * 512 + c * 128, 128)]))
            for h in range(2):
                u.append((('CG', h), [(w, 1 * 512 + h * 256, 256)]))
            for c in range(4):
                u.append((('BG', c), [(w, 3 * 512 + c * 128, 128), (w, 5 * 512 + c * 128, 128)]))
            wo = ('w_out_o', j)
        for h in range(4):
            u.append((('WO', h), [(wo, h * 256, 256)]))
        for h in range(4):
            u.append((('WG', h), [(('w_ple_gate', i), h * 256, 256)]))
        return u

    def build(self):
        nc = bass.Bass("TRN2", target_bir_lowering=False)
        nc.dge_precook = False
        self.nc = nc
        ntok = self.ntok
        dr = {}
        dr['xT'] = nc.dram_tensor("xT", [D, ntok], F32R, kind="ExternalInput").ap()
        dr['pT'] = nc.dram_tensor("pT", [DEPTH, 256, ntok], F32R, kind="ExternalInput").ap()
        for name, shp in (('w_in_e', [2, 1024, 3072]), ('w_in_o', [2, 1024, 3072]), ('w_out_e', [2, 1024, 1024]),
                          ('w_out_o', [2, 1024, 1024]), ('w_ple_gate', [4, 1024, 1024]), ('w_ple', [4, 256, 1024]),
                          ('w_pool', [2, 4, 128, 128]), ('onesr', [128, 256])):
            dr[name] = nc.dram_tensor(name, shp, F32R, kind="ExternalInput").ap()
        for name, shp in (('cols', [128, NCOL]), ('rows', [128, 2, 512]), ('wsT', [128, 2, 8, 128]),
                          ('bs', [128, 2, 4, 128]), ('cst', [128, NCST])):
            dr[name] = nc.dram_tensor(name, shp, F32, kind="ExternalInput").ap()
        dr['outT'] = nc.dram_tensor("outT", [D, ntok], F32, kind="ExternalOutput").ap()
        self.dr = dr

        with contextlib.ExitStack() as st:
            def sb(name, shape, dt):
                return st.enter_context(nc.sbuf_tensor("sb_" + name, shape, dt))
            self.x32 = sb("x32", [128, NCH, TT], F32)
            self.y32 = sb("y32", [128, NCH, TT], F32)
            self.pbuf = sb("pbuf", [128, 2, TT], F32)
            self.wple = sb("wple", [128, 2, 1024], F32)
            self.wring = sb("wring", [128, NSLOT, 8, 256], F32)
            ARENA = 24192
            self.arena = sb("arena", [128, ARENA], BF16)
            self.rows = sb("rows", [128, 2, 512], F32)
            self.onesb = sb("onesb", [128, 64], BF16)
            self.sm2 = sb("sm2", [128, 2, 8], F32)
            self.cols = sb("cols", [128, NCOL], F32)
            self.wsTb = sb("wsTb", [128, 2, 8, 128], BF16)
            self.bs = sb("bs", [128, 2, 4, 128], F32)
            self.wpool = sb("wpool", [128, 2, 4, 128], F32)
            self.cst = sb("cst", [128, NCST], F32)
            self.onesr = sb("onesr", [128, 256], F32)
            self.stA = sb("stA", [128, 2, 4, 30], BF16)
            self.stV = sb("stV", [128, 2, 4, 16], F32)
            self.stD = sb("stD", [128, 2, 4, 2], BF16)
            self.tr = [sb("tr%d" % k, [128, ST], F32) for k in range(NTR)]
            self.trr = [sb("trr%d" % k, [128, ST], F32) for k in range(NTRR)]
            self.pt = [sb("pt%d" % k, [128, 16 + ST], F32) for k in range(2)]
            self.sm = sb("sm", [128, 8, 8], F32)
            self.ps = [st.enter_context(nc.psum_tensor("ps%d" % k, [128, ST], F32)) for k in range(8)]
            ar = self.arena
            self.a_bf = ar[:, 0:4 * 1054].rearrange("p (c t) -> p c t", c=4)
            self.vln = ar[:, 4216:4216 + 4096].rearrange("p (b f) -> p b f", b=8)
            self.diag = ar[:, 8312:8312 + 124 * 128].rearrange("p (k n) -> p k n", k=124)
            self.vbuf = ar[:, 0:8320].bitcast(F32).rearrange("p (c t) -> p c t", c=4)
            self.dch = ar[:, 8320:8320 + 4 * 1026].rearrange("p (c t) -> p c t", c=4)
            self.diag3 = ar[:, 12424:12424 + 12 * 128].rearrange("p (k n) -> p k n", k=12)

            self.bufs = {'A': self.x32, 'B': self.y32}
            self.xn = 'A'
            self.yn = 'B'
            self.S = Sched(nc)
            self.gen()
            self.S.emit(st)
        return nc

    def T(self):
        k = self.tr_i % NTR
        self.tr_i += 1
        return self.tr[k], ('tr', k)

    def TR(self):
        k = self.trr_i % NTRR
        self.trr_i += 1
        return self.trr[k], ('trr', k)

    def bank(self, ring):
        lst = {'z': (0, 1, 2, 3), 'aux': (4, 5), 'st': (6, 7)}[ring]
        k = self.bank_i[ring] % len(lst)
        self.bank_i[ring] += 1
        b = lst[k]
        return self.ps[b], ('ps', b)

    def col(self, name, k=0):
        o = COLS[name] + k
        return self.cols[:, o:o + 1]

    def arena_extra(self):
        return list(self.prev_arena.items())

    def aop(self, e, fn, reads=(), writes=()):
        tok = self.S.op(e, fn, reads=reads, writes=writes, extra=self.arena_extra())
        if self.cur_arena.get(tok[0], 0) < tok[1]:
            self.cur_arena[tok[0]] = tok[1]
        return tok

    def arena_next_layer(self):
        for k, v in self.cur_arena.items():
            if self.prev_arena.get(k, 0) < v:
                self.prev_arena[k] = v
        self.cur_arena = {}

    def issue_upto(self, n):
        S = self.S
        n = min(n, len(self.units) - 1)
        while self.uissued <= n:
            m = self.uissued
            tag, segs = self.units[m]
            slot = m % NSLOT
            off = 0
            for (wname, widx), c0, nc_ in segs:
                src = self.dr[wname][widx].rearrange("(kc p) n -> p kc n", p=128)[:, :, c0:c0 + nc_]
                dst = self.wring[:, slot, :, off:off + nc_].bitcast(F32R)
                S.dma('sp', 'w%d' % slot, lambda e, dst=dst, src=src: e.dma_start(out=dst, in_=src),
                      writes=[('wslot', slot)])
                off += nc_
            self.uissued += 1

    def take_unit(self, tag, keep=False):
        if not keep:
            self.close_all()
        n = self.ucur
        assert self.units[n][0] == tag, (self.units[n][0], tag)
        assert self.uissued > n, "unit not issued yet"
        self.ucur += 1
        self.open.append(n)
        return n % NSLOT

    def close_unit(self, n):
        assert self.open[0] == n
        self.open.pop(0)
        self.issue_upto(n + NSLOT)

    def close_all(self):
        while self.open:
            self.close_unit(self.open[0])

    def xk(self, c, s):
        return (self.xn, c, s)

    def yk(self, c, s):
        return (self.yn, c, s)

    def mkview(self, name):
        buf = self.bufs[name]
        return lambda c, s: buf[:, c, s * ST:(s + 1) * ST]

    def zmm(self, slot, coff, s):
        bk, bkey = self.bank('z')
        X = self.mkview(self.xn)

        def fn(e, slot=slot, coff=coff, s=s, bk=bk):
            ins = None
            for k in range(NCH):
                ins = e.matmul(bk[:], self.wring[:, slot, k, coff:coff + 128].bitcast(F32R),
                               X(k, s).bitcast(F32R), start=(k == 0), stop=(k == NCH - 1))
            return ins
        self.S.op('pe', fn, reads=[('wslot', slot)] + [self.xk(k, s) for k in range(NCH)], writes=[bkey])
        return bk, bkey

    def gen(self):
        S = self.S
        self.tr_i = 0
        self.trr_i = 0
        self.bank_i = {'z': 0, 'aux': 0, 'st': 0}
        self.prev_arena = {}
        self.cur_arena = {}
        for k in range(NSLOT):
            S.new_sem('w%d' % k)
        for nm in ('ld_p', 'ld_wple'):
            S.new_sem(nm)
        for k in range(NCH * NSUB):
            S.new_sem('ld_x%d' % k)
        for k in range(NCH * NSUB):
            S.new_sem('st_out%d' % k)
        self.units = []
        for ti in range(self.ntiles):
            for i in self.layers:
                self.units.extend(self.layer_units(i))
        self.ucur = 0
        self.uissued = 0
        self.open = []
        self.issue_upto(NSLOT - 1)

        dr = self.dr

        def once(name, fn, writes):
            S.new_sem('ld_' + name)
            S.dma('pool', 'ld_' + name, fn, writes=writes)
        once('cols', lambda e: e.dma_start(out=self.cols[:], in_=dr['cols']), ['cols'])
        once('cst', lambda e: e.dma_start(out=self.cst[:], in_=dr['cst']), ['cst'])
        once('onesr', lambda e: e.dma_start(out=self.onesr[:].bitcast(F32R), in_=dr['onesr']), ['onesr'])
        once('bs', lambda e: e.dma_start(out=self.bs[:], in_=dr['bs']), ['bs'])
        once('rows', lambda e: e.dma_start(out=self.rows[:], in_=dr['rows']), ['rows'])
        once('wpool', lambda e: e.dma_start(out=self.wpool[:].bitcast(F32R),
                                            in_=dr['w_pool'].rearrange("j g c e -> c j g e")), ['wpool'])
        mask = self.cst[:, CST_MASK:CST_MASK + 128]
        for j in range(2):
            for q in range(2):
                t, tk = self.T()
                tv = t[:].rearrange("p (h t) -> p h t", h=4)
                once('wsT%d%d' % (j, q), lambda e, tv=tv, j=j, q=q: e.dma_start(out=tv, in_=dr['wsT'][:, j, q * 4:(q + 1) * 4, :]), [tk])
                S.op('dve', lambda e, tv=tv, j=j, q=q: e.tensor_tensor(
                    out=self.wsTb[:, j, q * 4:(q + 1) * 4, :], in0=tv,
                    in1=mask.unsqueeze(1).to_broadcast([128, 4, 128]), op=ALU.mult),
                    reads=[tk, 'cst'], writes=['wsTb'])

        S.op('dve', lambda e: e.memset(self.onesb[:], 1.0), writes=['onesb'])
        for j in range(2):
            bk, bkey = self.bank('z')

            def fn(e, j=j, bk=bk):
                ins = None
                for c in range(4):
                    for hh in range(2):
                        ins = e.matmul(bk[hh * 64:(hh + 1) * 64, c * 128:(c + 1) * 128], self.onesb[:, :],
                                       self.wsTb[:, j, 2 * c + hh, :], start=True, stop=True)
                return ins
            S.op('pe', fn, reads=['onesb', 'wsTb'], writes=[bkey])
            for c in range(4):
                S.op('dve', lambda e, j=j, c=c, bk=bk: e.scalar_tensor_tensor(
                    out=self.bs[:, j, c, :], in0=bk[:, c * 128:(c + 1) * 128], scalar=self.col('lvb%d' % j, c),
                    in1=self.bs[:, j, c, :], op0=ALU.mult, op1=ALU.add), reads=[bkey, 'cols', 'bs'], writes=['bs'])

        for ti in range(self.ntiles):
            self.ti = ti
            self.t0 = ti * TT
            self.seq_start = (ti % self.tps == 0)
            if ti == 0:
                for sx in range(NSUB):
                    self.gen_xload(0, sx)
            for li, i in enumerate(self.layers):
                self.last_layer = (li == len(self.layers) - 1)
                self.arena_next_layer()
                self.gen_layer(i)
        S.wait_tokens('pool', [('st_out%d' % k, S.cnt.get('st_out%d' % k, 0)) for k in range(NCH * NSUB)])

    def gen_xload(self, ti, sx):
        S = self.S
        dr = self.dr
        t0 = ti * TT
        xb = self.bufs[self.xn]
        for c in range(NCH):
            S.dma('pool', 'ld_x%d' % (c * NSUB + sx), lambda e, c=c, sx=sx, t0=t0, xb=xb: e.dma_start(
                out=xb[:, c, sx * ST:(sx + 1) * ST].bitcast(F32R),
                in_=dr['xT'][c * 128:(c + 1) * 128, t0 + sx * ST:t0 + (sx + 1) * ST]),
                writes=[self.xk(c, sx)])

    def gen_layer(self, i):
        S = self.S
        dr = self.dr
        j = i // 2
        t0 = self.t0
        S.dma('pool', 'ld_p', lambda e, i=i, t0=t0: e.dma_start(
            out=self.pbuf[:].bitcast(F32R), in_=dr['pT'][i].rearrange("(k p) t -> p k t", p=128)[:, :, t0:t0 + TT]),
            writes=['pbuf'])
        S.dma('pool', 'ld_wple', lambda e, i=i: e.dma_start(
            out=self.wple[:].bitcast(F32R), in_=dr['w_ple'][i].rearrange("(k p) n -> p k n", p=128)),
            writes=['wple'])
        if i % 2 == 0:
            self.gen_even(i, j)
        else:
            self.gen_odd(i, j)
        self.gen_tail(i, j)
        if not self.last_layer:
            self.xn, self.yn = self.yn, self.xn

    def gen_even(self, i, j):
        S = self.S
        X = self.mkview(self.xn)
        Y = self.mkview(self.yn)
        xbuf = self.bufs[self.xn]
        dr = self.dr
        bin_ = 'bin_e%d' % j
        ident = self.cst[:, CST_IDENT:CST_IDENT + 128]
        if self.seq_start:
            self.aop('pool', lambda e: e.memset(self.a_bf[:, :, 0:30], 0.0), writes=['abf_head'])
        else:
            self.aop('pool', lambda e, j=j: e.tensor_copy(out=self.a_bf[:, :, 0:30], in_=self.stA[:, j]),
                     reads=[('stA', j)], writes=['abf_head'])
        for idx in range(124):
            self.aop('pool', lambda e, idx=idx, j=j: e.tensor_scalar(
                out=self.diag[:, idx, :], in0=ident, scalar1=self.col('caw%d' % j, idx), scalar2=0.0,
                op0=ALU.mult, op1=ALU.add), reads=['cols', 'cst'], writes=[('diag', idx % 4)])

        for c in range(4):
            slot = self.take_unit(('A', c))
            for s in range(NSUB):
                bg, bgk = self.zmm(slot, 128, s)
                sg, sgk = self.T()
                S.op('act', lambda e, bg=bg, sg=sg, c=c: e.activation(
                    out=sg[:], in_=bg[:], func=AF.Sigmoid, bias=self.col(bin_, 4 + c), scale=1.0),
                    reads=[bgk, 'cols'], writes=[sgk])
                bv, bvk = self.zmm(slot, 0, s)
                self.aop('dve', lambda e, bv=bv, sg=sg, c=c, s=s: e.scalar_tensor_tensor(
                    out=self.a_bf[:, c, 30 + s * ST:30 + (s + 1) * ST], in0=bv[:], scalar=self.col(bin_, c),
                    in1=sg[:], op0=ALU.add, op1=ALU.mult),
                    reads=[bvk, sgk, 'cols'], writes=[('abf', c, s)])
        self.aop('pool', lambda e, j=j: e.tensor_copy(out=self.stA[:, j], in_=self.a_bf[:, :, TT:TT + 30]),
                 reads=[('abf', c, 1) for c in range(4)], writes=[('stA', j)])

        slot0 = self.take_unit(('V', 0))
        slot1 = self.take_unit(('V', 1), keep=True)
        BV = self.rows[:, j, :]
        for b in range(8):
            s = b // 4
            g = b // 4
            bk, bkey = self.bank('z')

            def fn(e, b=b, bk=bk):
                ins = None
                for h, sl in ((0, slot0), (1, slot1)):
                    for k in range(NCH):
                        ins = e.matmul(bk[:, h * 256:(h + 1) * 256],
                                       xbuf[:, k, b * 128:(b + 1) * 128].bitcast(F32R),
                                       self.wring[:, sl, k, :].bitcast(F32R), start=(k == 0), stop=(k == NCH - 1))
                return ins
            S.op('pe', fn, reads=[('wslot', slot0), ('wslot', slot1)] + [self.xk(k, s) for k in range(NCH)], writes=[bkey])
            t1, t1k = self.T()
            S.op('dve', lambda e, bk=bk, t1=t1: e.tensor_tensor(out=t1[:], in0=bk[:], in1=BV, op=ALU.add),
                 reads=[bkey, 'rows'], writes=[t1k])
            self.aop('act', lambda e, t1=t1, b=b: e.activation(out=self.vln[:, b, :], in_=t1[:], func=AF.Gelu_apprx_tanh),
                     reads=[t1k], writes=[('vln', b)])
            smb = self.sm[:, b, :]
            smk = ('sm', b)
            self.aop('dve', lambda e, b=b, smb=smb: e.bn_stats(out=smb[:, 0:6], in_=self.vln[:, b, :]),
                     reads=[('vln', b)], writes=[smk])
            S.op('dve', lambda e, smb=smb: e.bn_aggr(out=smb[:, 6:8], in_=smb[:, 0:6]), reads=[smk], writes=[smk])
            if b % 4 == 3:
                gk = ('sm2', g)
                blks = range(4 * g, 4 * g + 4)
                S.op('act', lambda e, g=g: e.activation(
                    out=self.sm2[:, g, 0:4], in_=self.sm[:, 4 * g:4 * g + 4, 7], func=AF.Sqrt,
                    bias=self.cst[:, CST_EPS:CST_EPS + 1], scale=1.0),
                    reads=[('sm', bb) for bb in blks] + ['cst'], writes=[gk])
                S.op('dve', lambda e, g=g: e.reciprocal(out=self.sm2[:, g, 4:8], in_=self.sm2[:, g, 0:4]),
                     reads=[gk], writes=[gk])
                for bi, bb in enumerate(blks):
                    self.aop('dve', lambda e, g=g, bi=bi, bb=bb: e.tensor_scalar(
                        out=self.vln[:, bb, :], in0=self.vln[:, bb, :], scalar1=self.sm[:, bb, 6:7],
                        scalar2=self.sm2[:, g, 4 + bi:5 + bi], op0=ALU.subtract, op1=ALU.mult),
                        reads=[gk, ('sm', bb), ('vln', bb)], writes=[('vln', bb)])

        ones512 = self.onesr[:, 128:256].bitcast(F32R)
        for s in range(NSUB):
            s1b, s1k = self.bank('st')
            s2b, s2k = self.bank('st')
            pend = []

            def stat_mm(c, sqt, sqk, s=s, s1b=s1b, s2b=s2b, s1k=s1k, s2k=s2k):
                def fn(e):
                    e.matmul(s1b[:], ones512, Y(c, s).bitcast(F32R), start=(c == 0), stop=(c == 3))
                    return e.matmul(s2b[:], ones512, sqt[:].bitcast(F32R), start=(c == 0), stop=(c == 3))
                S.op('pe', fn, reads=[self.yk(c, s), sqk, 'onesr'], writes=[s1k, s2k])
            for c in range(4):
                ab, abk = self.bank('aux')

                def fn(e, c=c, s=s, ab=ab):
                    ins = None
                    for k in range(31):
                        ins = e.matmul(ab[:], self.diag[:, k * 4 + c, :], self.a_bf[:, c, s * ST + k:s * ST + k + ST],
                                       start=(k == 0), stop=(k == 30))
                    return ins
                rd = [('diag', c), ('abf', c, s), ('abf', c, s - 1) if s > 0 else 'abf_head']
                self.aop('pe', fn, reads=rd, writes=[abk])
                S.op('act', lambda e, c=c, s=s, ab=ab: e.activation(
                    out=Y(c, s).bitcast(F32R), in_=ab[:], func=AF.Identity, bias=self.col('cab%d' % j, c), scale=1.0),
                    reads=[abk, 'cols'], writes=[self.yk(c, s)])
                sq, sqk = self.TR()
                S.op('act', lambda e, c=c, ab=ab, sq=sq: e.activation(
                    out=sq[:].bitcast(F32R), in_=ab[:], func=AF.Square, bias=self.col('cab%d' % j, c), scale=1.0),
                    reads=[abk, 'cols'], writes=[sqk])
                pend.append((c, sq, sqk))
                if len(pend) > 1:
                    stat_mm(*pend.pop(0))
            while pend:
                stat_mm(*pend.pop(0))
            def ln_a(s=s, s1b=s1b, s1k=s1k, s2b=s2b, s2k=s2k):
                self.ln_small(s1b, s1k, s2b, s2k)
                yield
                for c in range(4):
                    t, tk = self.T()
                    S.op('dve', lambda e, c=c, s=s, t=t, s2b=s2b: e.tensor_tensor(
                        out=t[:], in0=Y(c, s).bitcast(F32), in1=s2b[:], op=ALU.mult), reads=[self.yk(c, s), s2k], writes=[tk])
                    S.op('dve', lambda e, t=t, s1b=s1b: e.tensor_tensor(out=t[:], in0=t[:], in1=s1b[:], op=ALU.add),
                         reads=[tk, s1k], writes=[tk])
                    S.op('act', lambda e, c=c, s=s, t=t: e.activation(
                        out=Y(c, s).bitcast(F32R), in_=t[:], func=AF.Silu, bias=self.col('lab%d' % j, c),
                        scale=self.col('lag%d' % j, c)), reads=[tk, 'cols'], writes=[self.yk(c, s)])
                    yield
            if s == 0:
                for _ in ln_a():
                    pass
            else:
                deferred_ln = ln_a()

        for c in range(4):
            slot = self.take_unit(('UG', c))
            bus = []
            abs_ = []
            bgs = []
            for s in range(NSUB):
                bus.append(self.zmm(slot, 0, s))
            for s in range(NSUB):
                ab, abk = self.bank('aux')

                def fn(e, c=c, s=s, ab=ab):
                    ins = None
                    for hh in range(2):
                        h = 2 * c + hh
                        for bb in range(4):
                            b = s * 4 + bb
                            ins = e.matmul(ab[hh * 64:(hh + 1) * 64, bb * 128:(bb + 1) * 128],
                                           self.vln[:, b, h * 64:(h + 1) * 64], self.wsTb[:, j, h, :],
                                           start=True, stop=True)
                    return ins
                self.aop('pe', fn, reads=[('vln', s * 4 + bb) for bb in range(4)] + ['wsTb'], writes=[abk])
                abs_.append((ab, abk))
            for s in range(NSUB):
                bgs.append(self.zmm(slot, 128, s))
            t1s = []
            t2s = []
            for s in range(NSUB):
                bu, buk = bus[s]
                t1, t1k = self.T()
                S.op('act', lambda e, bu=bu, t1=t1, c=c: e.activation(
                    out=t1[:], in_=bu[:], func=AF.Gelu_apprx_tanh, bias=self.col(bin_, 12 + c), scale=1.0),
                    reads=[buk, 'cols'], writes=[t1k])
                t1s.append((t1, t1k))
            for s in range(NSUB):
                bgt, bgk = bgs[s]
                t2, t2k = self.T()
                S.op('act', lambda e, bgt=bgt, t2=t2, c=c: e.activation(
                    out=t2[:], in_=bgt[:], func=AF.Silu, bias=self.col(bin_, 20 + c), scale=1.0),
                    reads=[bgk, 'cols'], writes=[t2k])
                t2s.append((t2, t2k))
            t3s = []
            for s in range(NSUB):
                ab, abk = abs_[s]
                t3, t3k = self.T()
                S.op('dve', lambda e, ab=ab, t3=t3, c=c: e.scalar_tensor_tensor(
                    out=t3[:].rearrange("p (b t) -> p b t", b=4), in0=ab[:].rearrange("p (b t) -> p b t", b=4),
                    scalar=self.col('lvg%d' % j, c),
                    in1=self.bs[:, j, c, :].unsqueeze(1).to_broadcast([128, 4, 128]), op0=ALU.mult, op1=ALU.add),
                    reads=[abk, 'bs', 'cols'], writes=[t3k])
                t3s.append((t3, t3k))
            for s in range(NSUB):
                t1, t1k = t1s[s]
                t3, t3k = t3s[s]
                S.op('dve', lambda e, t1=t1, t3=t3: e.tensor_tensor(out=t3[:], in0=t3[:], in1=t1[:], op=ALU.mult),
                     reads=[t1k, t3k], writes=[t3k])
            for s in range(NSUB):
                t2, t2k = t2s[s]
                t3, t3k = t3s[s]
                S.op('dve', lambda e, t2=t2, t3=t3, c=c, s=s: e.tensor_tensor(
                    out=Y(4 + c, s).bitcast(F32R), in0=t3[:], in1=t2[:], op=ALU.mult),
                    reads=[t2k, t3k], writes=[self.yk(4 + c, s)])
            for _ in range(2):
                next(deferred_ln, None)
        for _ in deferred_ln:
            pass

        for h in range(2):
            slot = self.take_unit(('AG', h))
            for s in range(NSUB):
                for cc in range(2):
                    c = 2 * h + cc
                    bk, bkey = self.zmm(slot, cc * 128, s)
                    t, tk = self.T()
                    S.op('act', lambda e, bk=bk, t=t, c=c: e.activation(
                        out=t[:], in_=bk[:], func=AF.Silu, bias=self.col(bin_, 8 + c), scale=1.0),
                        reads=[bkey, 'cols'], writes=[tk])
                    S.op('dve', lambda e, t=t, c=c, s=s: e.tensor_tensor(
                        out=Y(c, s).bitcast(F32R), in0=Y(c, s).bitcast(F32), in1=t[:], op=ALU.mult),
                        reads=[tk, self.yk(c, s)], writes=[self.yk(c, s)])

    def ln_small(self, s1b, s1k, s2b, s2k):
        S = self.S
        t, tk = self.T()
        S.op('act', lambda e, t=t, s1b=s1b: e.activation(out=t[:], in_=s1b[:], func=AF.Square), reads=[s1k], writes=[tk])
        S.op('dve', lambda e, t=t, s2b=s2b: e.tensor_tensor(out=t[:], in0=s2b[:], in1=t[:], op=ALU.subtract),
             reads=[s2k, tk], writes=[tk])
        S.op('act', lambda e, t=t: e.activation(out=t[:], in_=t[:], func=AF.Ln, bias=self.cst[:, CST_EPS:CST_EPS + 1], scale=1.0),
             reads=[tk, 'cst'], writes=[tk])
        S.op('act', lambda e, t=t: e.activation(out=t[:], in_=t[:], func=AF.Exp, scale=-0.5),
             reads=[tk], writes=[tk])
        S.op('act', lambda e, t=t, s2b=s2b: e.activation(out=s2b[:], in_=t[:], func=AF.Identity),
             reads=[tk], writes=[s2k])
        S.op('dve', lambda e, t=t, s1b=s1b: e.scalar_tensor_tensor(out=s1b[:], in0=s1b[:], scalar=-1.0, in1=t[:],
                                                                   op0=ALU.mult, op1=ALU.mult),
             reads=[tk, s1k], writes=[s1k])

    def gen_odd(self, i, j):
        S = self.S
        X = self.mkview(self.xn)
        Y = self.mkview(self.yn)
        xbuf = self.bufs[self.xn]
        bin_ = 'bin_o%d' % j
        ident = self.cst[:, CST_IDENT:CST_IDENT + 128]
        if self.seq_start:
            self.aop('pool', lambda e: e.memset(self.vbuf[:, :, 0:16], 0.0), writes=['vb_head'])
            self.aop('pool', lambda e: e.memset(self.dch[:, :, 0:2], 0.0), writes=['dch_head'])
        else:
            self.aop('pool', lambda e, j=j: e.tensor_copy(out=self.vbuf[:, :, 0:16], in_=self.stV[:, j]),
                     reads=[('stV', j)], writes=['vb_head'])
            self.aop('pool', lambda e, j=j: e.tensor_copy(out=self.dch[:, :, 0:2], in_=self.stD[:, j]),
                     reads=[('stD', j)], writes=['dch_head'])
        for idx in range(12):
            self.aop('pool', lambda e, idx=idx, j=j: e.tensor_scalar(
                out=self.diag3[:, idx, :], in0=ident, scalar1=self.col('cdw%d' % j, idx), scalar2=0.0,
                op0=ALU.mult, op1=ALU.add), reads=['cols', 'cst'], writes=[('diag3', idx % 4)])

        for h in range(2):
            slot = self.take_unit(('CV', h))
            for s in range(NSUB):
                for cc in range(2):
                    c = 2 * h + cc
                    bk, bkey = self.zmm(slot, cc * 128, s)
                    self.aop('act', lambda e, bk=bk, c=c, s=s: e.activation(
                        out=self.vbuf[:, c, 16 + s * ST:16 + (s + 1) * ST], in_=bk[:], func=AF.Identity,
                        bias=self.col(bin_, c), scale=1.0), reads=[bkey, 'cols'], writes=[('vb', c, s)])
        self.aop('pool', lambda e, j=j: e.tensor_copy(out=self.stV[:, j], in_=self.vbuf[:, :, TT:TT + 16]),
                 reads=[('vb', c, 1) for c in range(4)], writes=[('stV', j)])

        def pool_dve(c, s):
            W = POOL_WINDOWS[c]
            lo = s * ST
            src = self.vbuf[:, c, lo:lo + 16 + ST]
            rdk = [('vb', c, s), ('vb', c, s - 1) if s > 0 else 'vb_head']
            cur = src
            curk = None
            sh = 1
            nstep = c + 1
            for stp in range(nstep):
                dst = self.pt[stp % 2]
                dk = ('pt', stp % 2)
                a0 = 16 - W + 2 * sh
                n = 16 + ST - a0
                rds = list(rdk) if curk is None else [curk]
                self.aop('dve', lambda e, dst=dst, cur=cur, a0=a0, n=n, sh=sh: e.tensor_tensor(
                    out=dst[:, a0:a0 + n], in0=cur[:, a0:a0 + n], in1=cur[:, a0 - sh:a0 - sh + n], op=ALU.add),
                    reads=rds, writes=[dk])
                cur = dst
                curk = dk
                sh *= 2
            pl, plk = self.TR()
            self.aop('dve', lambda e, cur=cur, pl=pl, src=src, W=W: e.scalar_tensor_tensor(
                out=pl[:].bitcast(F32R), in0=cur[:, 16:16 + ST], scalar=1.0 / W, in1=src[:, 16:16 + ST],
                op0=ALU.mult, op1=ALU.subtract), reads=[curk] + rdk, writes=[plk])
            if self.seq_start and s == 0:
                inv = self.cst[:, CST_INV + c * 16:CST_INV + (c + 1) * 16]
                self.aop('dve', lambda e, cur=cur, inv=inv: e.tensor_tensor(
                    out=cur[:, 16:32], in0=cur[:, 16:32], in1=inv, op=ALU.mult), reads=[curk, 'cst'], writes=[curk])
                self.aop('dve', lambda e, cur=cur, pl=pl, src=src: e.tensor_tensor(
                    out=pl[:, 0:16].bitcast(F32R), in0=cur[:, 16:32], in1=src[:, 16:32], op=ALU.subtract),
                    reads=[curk] + rdk, writes=[plk])
            return pl, plk

        def pool_pe(c, s, pl, plk):
            ab, abk = self.bank('aux')
            S.op('pe', lambda e, ab=ab, pl=pl, c=c: e.matmul(
                ab[:], self.wpool[:, j, c, :].bitcast(F32R), pl[:].bitcast(F32R), start=True, stop=True),
                reads=[plk, 'wpool'], writes=[abk])
            S.op('act', lambda e, ab=ab, c=c, s=s: e.activation(
                out=Y(c, s).bitcast(F32R), in_=ab[:], func=AF.Identity, scale=self.col('psc%d' % j, c)),
                reads=[abk, 'cols'], writes=[self.yk(c, s)])

        for c in range(4):
            pls = [pool_dve(c, s) for s in range(NSUB)]
            slot = self.take_unit(('HC', c))
            for s in range(NSUB):
                bh, bhk = self.zmm(slot, 0, s)
                t, tk = self.T()
                S.op('act', lambda e, bh=bh, t=t, c=c: e.activation(
                    out=t[:], in_=bh[:], func=AF.Identity, bias=self.col(bin_, 8 + c), scale=1.0),
                    reads=[bhk, 'cols'], writes=[tk])
                bc, bck = self.zmm(slot, 128, s)
                self.aop('dve', lambda e, bc=bc, t=t, c=c, s=s: e.scalar_tensor_tensor(
                    out=self.dch[:, c, 2 + s * ST:2 + (s + 1) * ST], in0=bc[:], scalar=self.col(bin_, 16 + c),
                    in1=t[:], op0=ALU.add, op1=ALU.mult), reads=[bck, tk, 'cols'], writes=[('dch', c, s)])
            for s in range(NSUB):
                pool_pe(c, s, *pls[s])
        self.aop('pool', lambda e, j=j: e.tensor_copy(out=self.stD[:, j], in_=self.dch[:, :, TT:TT + 2]),
                 reads=[('dch', c, 1) for c in range(4)], writes=[('stD', j)])

        for s in range(NSUB):
            for c in range(4):
                ab, abk = self.bank('aux')

                def fn(e, c=c, s=s, ab=ab):
                    ins = None
                    for k in range(3):
                        ins = e.matmul(ab[:], self.diag3[:, k * 4 + c, :], self.dch[:, c, s * ST + k:s * ST + k + ST],
                                       start=(k == 0), stop=(k == 2))
                    return ins
                rd = [('diag3', c), ('dch', c, s), ('dch', c, s - 1) if s > 0 else 'dch_head']
                self.aop('pe', fn, reads=rd, writes=[abk])
                S.op('act', lambda e, ab=ab, c=c, s=s: e.activation(
                    out=Y(4 + c, s).bitcast(F32R), in_=ab[:], func=AF.Identity),
                    reads=[abk], writes=[self.yk(4 + c, s)])

        for h in range(2):
            slot = self.take_unit(('CG', h))
            for s in range(NSUB):
                for cc in range(2):
                    c = 2 * h + cc
                    bk, bkey = self.zmm(slot, cc * 128, s)
                    t, tk = self.T()
                    S.op('act', lambda e, bk=bk, t=t, c=c: e.activation(
                        out=t[:], in_=bk[:], func=AF.Silu, bias=self.col(bin_, 4 + c), scale=1.0),
                        reads=[bkey, 'cols'], writes=[tk])
                    S.op('dve', lambda e, t=t, c=c, s=s: e.tensor_tensor(
                        out=Y(c, s).bitcast(F32R), in0=Y(c, s).bitcast(F32), in1=t[:], op=ALU.mult),
                        reads=[tk, self.yk(c, s)], writes=[self.yk(c, s)])

        for c in range(4):
            slot = self.take_unit(('BG', c))
            for s in range(NSUB):
                bb_, bbk = self.zmm(slot, 0, s)
                bg, bgk = self.zmm(slot, 128, s)
                t1, t1k = self.T()
                S.op('act', lambda e, bg=bg, t1=t1, c=c: e.activation(
                    out=t1[:], in_=bg[:], func=AF.Silu, bias=self.col(bin_, 20 + c), scale=1.0),
                    reads=[bgk, 'cols'], writes=[t1k])
                S.op('dve', lambda e, bb_=bb_, t1=t1, c=c: e.scalar_tensor_tensor(
                    out=t1[:], in0=bb_[:], scalar=self.col(bin_, 12 + c), in1=t1[:], op0=ALU.add, op1=ALU.mult),
                    reads=[bbk, t1k, 'cols'], writes=[t1k])
                S.op('dve', lambda e, t1=t1, c=c, s=s: e.tensor_tensor(
                    out=Y(4 + c, s).bitcast(F32R), in0=Y(4 + c, s).bitcast(F32), in1=t1[:], op=ALU.mult),
                    reads=[t1k, self.yk(4 + c, s)], writes=[self.yk(4 + c, s)])

    def gen_tail(self, i, j):
        S = self.S
        X = self.mkview(self.xn)
        Y = self.mkview(self.yn)
        xbuf = self.bufs[self.xn]
        dr = self.dr
        bout = ('bout_e%d' if i % 2 == 0 else 'bout_o%d') % j
        ones1024 = self.onesr[:, 0:128].bitcast(F32R)
        stb = {}
        for s in range(NSUB):
            ring = 'st' if s == 0 else 'aux'
            b1, k1 = self.bank(ring)
            b2, k2 = self.bank(ring)
            stb[s] = (b1, k1, b2, k2)
        pend = []

        def stat_mm(c, s, sqt, sqk):
            b1, k1, b2, k2 = stb[s]

            def fn(e):
                e.matmul(b1[:], ones1024, X(c, s).bitcast(F32R), start=(c == 0), stop=(c == NCH - 1))
                return e.matmul(b2[:], ones1024, sqt[:].bitcast(F32R), start=(c == 0), stop=(c == NCH - 1))
            S.op('pe', fn, reads=[self.xk(c, s), sqk, 'onesr'], writes=[k1, k2])
        wo_slots = []
        wo_n = []
        for h in range(4):
            wo_n.append(self.ucur)
            wo_slots.append(self.take_unit(('WO', h), keep=(h > 0)))

        def flush():
            while pend:
                stat_mm(*pend.pop(0))

        def wo_pass(s):
            for h in range(4):
                slot = wo_slots[h]
                for cc in range(2):
                    c = 2 * h + cc
                    for ca in ((0, 1, 2) if c == 0 else (c + 2,)):
                        if ca < NCH:
                            S.op('act', lambda e, ca=ca, s=s: e.activation(
                                out=X(ca, s).bitcast(F32R), in_=X(ca, s), func=AF.Identity, bias=self.col(bout, ca),
                                scale=ALPHA), reads=[self.xk(ca, s), 'cols'], writes=[self.xk(ca, s)])
                    bk, bkey = self.bank('z')

                    def fn(e, slot=slot, cc=cc, s=s, bk=bk):
                        ins = None
                        for k in range(NCH):
                            ins = e.matmul(bk[:], self.wring[:, slot, k, cc * 128:(cc + 1) * 128].bitcast(F32R),
                                           Y(k, s).bitcast(F32R), start=(k == 0), stop=(k == NCH - 1))
                        return ins
                    S.op('pe', fn, reads=[('wslot', slot)] + [self.yk(k, s) for k in range(NCH)], writes=[bkey])
                    S.op('dve', lambda e, bk=bk, c=c, s=s: e.tensor_tensor(
                        out=X(c, s).bitcast(F32R), in0=bk[:], in1=X(c, s), op=ALU.add),
                        reads=[bkey, self.xk(c, s)], writes=[self.xk(c, s)])
                    sq, sqk = self.TR()
                    S.op('act', lambda e, sq=sq, c=c, s=s: e.activation(
                        out=sq[:].bitcast(F32R), in_=X(c, s), func=AF.Square), reads=[self.xk(c, s)], writes=[sqk])
                    pend.append((c, s, sq, sqk))
                    if len(pend) > 2:
                        stat_mm(*pend.pop(0))
                    yield
                if s == NSUB - 1:
                    self.close_unit(wo_n[h])

        def run(g):
            for _ in g:
                pass

        def interleave(ga, gb, pattern=()):
            da = db = False
            step = 0
            while not (da and db):
                if not da:
                    try:
                        next(ga)
                    except StopIteration:
                        da = True
                nb = pattern[step] if step < len(pattern) else 1
                step += 1
                for _ in range(nb):
                    if not db:
                        try:
                            next(gb)
                        except StopIteration:
                            db = True

        run(wo_pass(0))
        flush()
        interleave(wo_pass(1), self.gen_ln_apply(i, 0, stb, X))
        flush()

        wg_slots = []
        wg_n = []
        for h in range(4):
            wg_n.append(self.ucur)
            wg_slots.append(self.take_unit(('WG', h), keep=(h > 0)))

        def gate_pass(s):
            for h in range(4):
                slot = wg_slots[h]
                for cc in range(2):
                    c = 2 * h + cc
                    bk, bkey = self.zmm(slot, cc * 128, s)
                    gt, gtk = self.T()
                    S.op('act', lambda e, bk=bk, gt=gt, c=c: e.activation(
                        out=gt[:], in_=bk[:], func=AF.Sigmoid, bias=self.col('bpg%d' % i, c), scale=1.0),
                        reads=[bkey, 'cols'], writes=[gtk])
                    bp, bpk = self.bank('st')

                    def fn(e, c=c, s=s, bp=bp):
                        ins = None
                        for k in range(2):
                            ins = e.matmul(bp[:], self.wple[:, k, c * 128:(c + 1) * 128].bitcast(F32R),
                                           self.pbuf[:, k, s * ST:(s + 1) * ST].bitcast(F32R), start=(k == 0), stop=(k == 1))
                        return ins
                    S.op('pe', fn, reads=['wple', 'pbuf'], writes=[bpk])
                    S.op('dve', lambda e, bp=bp, gt=gt: e.tensor_tensor(out=gt[:], in0=bp[:], in1=gt[:], op=ALU.mult),
                         reads=[bpk, gtk], writes=[gtk])
                    S.op('dve', lambda e, gt=gt, c=c, s=s: e.tensor_tensor(
                        out=Y(c, s).bitcast(F32R), in0=X(c, s), in1=gt[:], op=ALU.add),
                        reads=[gtk, self.xk(c, s)], writes=[self.yk(c, s)])
                    if self.last_layer:
                        t0 = self.t0
                        S.dma('pool', 'st_out%d' % (c * NSUB + s), lambda e, c=c, s=s, t0=t0: e.dma_start(
                            out=dr['outT'][c * 128:(c + 1) * 128, t0 + s * ST:t0 + (s + 1) * ST], in_=Y(c, s)),
                            reads=[self.yk(c, s)])
                    yield
                if s == NSUB - 1:
                    self.close_unit(wg_n[h])

        interleave(gate_pass(0), self.gen_ln_apply(i, 1, stb, X), pattern=(2, 2, 2, 2, 2))
        if self.last_layer and self.ti + 1 < self.ntiles:
            self.gen_xload(self.ti + 1, 0)
        run(gate_pass(1))
        if self.last_layer and self.ti + 1 < self.ntiles:
            self.gen_xload(self.ti + 1, 1)

    def gen_ln_apply(self, i, s, stb, X):
        S = self.S
        b1, k1, b2, k2 = stb[s]
        self.ln_small(b1, k1, b2, k2)
        yield
        for c in range(NCH):
            t, tk = self.T()
            S.op('dve', lambda e, c=c, s=s, t=t, b2=b2: e.tensor_tensor(
                out=t[:], in0=X(c, s), in1=b2[:], op=ALU.mult), reads=[self.xk(c, s), k2], writes=[tk])
            S.op('dve', lambda e, t=t, b1=b1: e.tensor_tensor(out=t[:], in0=t[:], in1=b1[:], op=ALU.add),
                 reads=[tk, k1], writes=[tk])
            S.op('act', lambda e, c=c, s=s, t=t: e.activation(
                out=X(c, s).bitcast(F32R), in_=t[:], func=AF.Identity, bias=self.col('lnb%d' % i, c),
                scale=self.col('lng%d' % i, c)), reads=[tk, 'cols'], writes=[self.xk(c, s)])
            yield


_NC_CACHE = {}


def get_nc(ntiles, tps, layers):
    key = (ntiles, tps, tuple(layers))
    if key not in _NC_CACHE:
        _NC_CACHE[key] = Builder(ntiles, tps, layers).build()
    return _NC_CACHE[key]


def run_cores(inp, x_cores, p_cores, ntiles, tps, layers):
    shared = host_prep(inp)
    nc = get_nc(ntiles, tps, layers)
    in_maps = []
    for xc, pc in zip(x_cores, p_cores):
        m = dict(shared)
        m['xT'] = np.ascontiguousarray(np.asarray(xc, np.float32).T)
        m['pT'] = np.ascontiguousarray(np.asarray(pc, np.float32).transpose(0, 2, 1))
        in_maps.append(m)
    res = run_bass_kernel_spmd(nc, in_maps, core_ids=list(range(len(in_maps))))
    return [np.ascontiguousarray(r['outT'].T) for r in res.results]


def kernel(**inputs):
    x = np.asarray(inputs['x'], np.float32)
    p = np.asarray(inputs['p'], np.float32)
    B, SEQ, _ = x.shape
    bpc = B // NCORES
    x_cores = [x[k * bpc:(k + 1) * bpc].reshape(bpc * SEQ, D) for k in range(NCORES)]
    p_cores = [p[:, k * bpc:(k + 1) * bpc].reshape(DEPTH, bpc * SEQ, 256) for k in range(NCORES)]
    outs = run_cores(inputs, x_cores, p_cores, ntiles=bpc * SEQ // TT, tps=SEQ // TT, layers=range(DEPTH))
    out = np.stack([o.reshape(bpc, SEQ, D) for o in outs], axis=0).reshape(B, SEQ, D)
    return out.astype(np.float32)
```
